# Optimizing a Trainium2 kernel written in Bass

```python
import math
import jax
import jax.numpy as jnp
from jax import lax
import numpy as np

D_MODEL = 1024
BATCH = 8
SEQ = 4096
DEPTH = 1
DEC_BATCH = 8
DEC_SEQ = 64
PAST_LEN = 4096

CHUNK = 64
Q_BLOCK = 128
MIX_WIDTH = D_MODEL
RET_WIDTH = MIX_WIDTH // 2
DIFF_WIDTH = MIX_WIDTH - RET_WIDTH
RET_HEADS = 4
RET_DK = RET_WIDTH // RET_HEADS
RET_DV = RET_WIDTH // RET_HEADS
DIFF_HEADS = 4
DIFF_DV = DIFF_WIDTH // DIFF_HEADS
DIFF_DK = DIFF_DV // 2
ROT_DIM = DIFF_DK // 4
ROPE_THETA = 500000.0
RET_THETA = 10000.0
D_FF = 2816
EPS = 1e-6
IN_SPLITS = [RET_WIDTH, 2 * RET_WIDTH, 3 * RET_WIDTH, 4 * RET_WIDTH,
             4 * RET_WIDTH + DIFF_WIDTH, 4 * RET_WIDTH + 2 * DIFF_WIDTH]
IN_COLS = 4 * RET_WIDTH + 3 * DIFF_WIDTH

kernel_name = "hybrid_retention_diffattn_macaron_stream_step"


def _rmsnorm(x, g):
    xf = x.astype(jnp.float32)
    y = xf * lax.rsqrt(jnp.mean(xf * xf, axis=-1, keepdims=True) + EPS)
    return (y * g.astype(jnp.float32)).astype(x.dtype)


def _head_layernorm(o, g):
    of = o.astype(jnp.float32)
    mu = jnp.mean(of, axis=-1, keepdims=True)
    var = jnp.mean((of - mu) ** 2, axis=-1, keepdims=True)
    return ((of - mu) * lax.rsqrt(var + EPS) * g.astype(jnp.float32)).astype(o.dtype)


def _swiglu(h, wg, wu, wd):
    return (jax.nn.silu(h @ wg) * (h @ wu)) @ wd


def _rope(x, pos, theta, rot_dim):
    half = rot_dim // 2
    inv = jnp.power(jnp.float32(theta), -jnp.arange(half, dtype=jnp.float32) * (2.0 / rot_dim))
    ang = pos.astype(jnp.float32)[:, None] * inv[None, :]
    ang = ang.reshape((1, x.shape[1]) + (1,) * (x.ndim - 3) + (half,))
    cos, sin = jnp.cos(ang), jnp.sin(ang)
    xf = x.astype(jnp.float32)
    x1, x2, rest = xf[..., :half], xf[..., half:rot_dim], xf[..., rot_dim:]
    out = jnp.concatenate([x1 * cos - x2 * sin, x2 * cos + x1 * sin, rest], axis=-1)
    return out.astype(x.dtype)


def _retention(q, k, v, s0, chunk):
    b, l, h, dk = q.shape
    dv = v.shape[-1]
    n = l // chunk
    log_g = jnp.log(1.0 - jnp.power(2.0, -5.0 - jnp.arange(h, dtype=jnp.float32)))
    idx = jnp.arange(chunk, dtype=jnp.float32)
    rel = idx[:, None] - idx[None, :]
    dmask = jnp.where(rel >= 0, jnp.exp(log_g[:, None, None] * jnp.maximum(rel, 0.0)), 0.0)
    qf = q.astype(jnp.float32).reshape(b, n, chunk, h, dk)
    kf = k.astype(jnp.float32).reshape(b, n, chunk, h, dk)
    vf = v.astype(jnp.float32).reshape(b, n, chunk, h, dv)
    scores = jnp.einsum('bnihd,bnjhd->bnhij', qf, kf) * dmask
    o = jnp.einsum('bnhij,bnjhe->bnihe', scores, vf)
    k_dec = jnp.exp(log_g[None, :] * (chunk - 1.0 - idx)[:, None])
    kv = jnp.einsum('bnjhd,bnjhe->nbhde', kf * k_dec[:, :, None], vf)
    blk_dec = jnp.exp(log_g * chunk)[None, :, None, None]

    def step(s, kv_c):
        return blk_dec * s + kv_c, s

    s_last, s_before = lax.scan(step, s0.astype(jnp.float32), kv)
    q_dec = jnp.exp(log_g[None, :] * (idx + 1.0)[:, None])
    o = o + jnp.einsum('bnihd,nbhde->bnihe', qf * q_dec[:, :, None], s_before)
    return o.reshape(b, l, h, dv).astype(q.dtype), s_last


def _diff_attn(q, k, v, q_pos, k_pos, lam):
    s = jnp.einsum('bqhcd,bkhcd->bhcqk', q, k).astype(jnp.float32) * (DIFF_DK ** -0.5)
    mask = (k_pos[None, :] // CHUNK) <= (q_pos[:, None] // CHUNK)
    s = jnp.where(mask, s, -1e30)
    p = jax.nn.softmax(s, axis=-1)
    a = p[:, :, 0] - lam * p[:, :, 1]
    return jnp.einsum('bhqk,bkhe->bqhe', a.astype(v.dtype), v)


def _layer(x, ret_s0, k_past, v_past, f1_pre, f1_wg, f1_wu, f1_wd, f1_post, mix_pre, w_in,
           ret_g, lq1, lk1, lq2, lk2, diff_g, w_out, mix_post, f2_pre, f2_wg, f2_wu, f2_wd,
           f2_post, lam_init):
    b, l, _ = x.shape
    p_len = k_past.shape[1]
    pos = p_len + jnp.arange(l, dtype=jnp.int32)
    x = x + 0.5 * _rmsnorm(_swiglu(_rmsnorm(x, f1_pre), f1_wg, f1_wu, f1_wd), f1_post)
    h = _rmsnorm(x, mix_pre)
    q_r, k_r, v_r, g_r, q_d, k_d, v_d = jnp.split(h @ w_in, IN_SPLITS, axis=-1)
    q_r = _rope(q_r.reshape(b, l, RET_HEADS, RET_DK), pos, RET_THETA, RET_DK)
    k_r = _rope(k_r.reshape(b, l, RET_HEADS, RET_DK), pos, RET_THETA, RET_DK) * (RET_DK ** -0.5)
    v_r = v_r.reshape(b, l, RET_HEADS, RET_DV)
    ret_o, s_new = _retention(q_r, k_r, v_r, ret_s0, min(CHUNK, l))
    ret_y = jax.nn.silu(g_r) * _head_layernorm(ret_o, ret_g).reshape(b, l, RET_WIDTH)
    q_d = _rope(q_d.reshape(b, l, DIFF_HEADS, 2, DIFF_DK), pos, ROPE_THETA, ROT_DIM)
    k_d = _rope(k_d.reshape(b, l, DIFF_HEADS, 2, DIFF_DK), pos, ROPE_THETA, ROT_DIM)
    v_d = v_d.reshape(b, l, DIFF_HEADS, DIFF_DV)
    lam = (jnp.exp(jnp.sum(lq1.astype(jnp.float32) * lk1.astype(jnp.float32)))
           - jnp.exp(jnp.sum(lq2.astype(jnp.float32) * lk2.astype(jnp.float32))) + lam_init)
    k_all = jnp.concatenate([k_past.astype(k_d.dtype), k_d], axis=1)
    v_all = jnp.concatenate([v_past.astype(v_d.dtype), v_d], axis=1)
    k_pos = jnp.arange(p_len + l, dtype=jnp.int32)
    qb = min(Q_BLOCK, l)
    outs = []
    for blk in range(l // qb):
        end = p_len + (blk + 1) * qb
        outs.append(_diff_attn(q_d[:, blk * qb:(blk + 1) * qb], k_all[:, :end], v_all[:, :end],
                               pos[blk * qb:(blk + 1) * qb], k_pos[:end], lam))
    diff_o = jnp.concatenate(outs, axis=1)
    diff_y = (_rmsnorm(diff_o, diff_g) * (1.0 - lam_init)).reshape(b, l, DIFF_WIDTH)
    y = jnp.concatenate([ret_y, diff_y], axis=-1) @ w_out
    x = x + _rmsnorm(y, mix_post)
    x = x + 0.5 * _rmsnorm(_swiglu(_rmsnorm(x, f2_pre), f2_wg, f2_wu, f2_wd), f2_post)
    return x, s_new, k_d, v_d


def setup_inputs(seed: int = 0) -> dict:
    key = jax.random.key(seed)
    ks = iter(jax.random.split(key, 32))

    def nrm(shape, scale):
        return jax.random.normal(next(ks), shape, jnp.float32) * scale

    def gain(shape):
        return 1.0 + nrm(shape, 0.05)

    return {
        'x_prompt': nrm((BATCH, SEQ, D_MODEL), 1.0),
        'x_sample': nrm((DEC_BATCH, DEC_SEQ, D_MODEL), 1.0),
        'state_ret': nrm((DEPTH, DEC_BATCH, RET_HEADS, RET_DK, RET_DV), 1.0),
        'cache_diff_k': nrm((DEPTH, DEC_BATCH, PAST_LEN, DIFF_HEADS, 2, DIFF_DK), 1.0),
        'cache_diff_v': nrm((DEPTH, DEC_BATCH, PAST_LEN, DIFF_HEADS, DIFF_DV), 1.0),
        'ffn1_pre_g': gain((DEPTH, D_MODEL)),
        'ffn1_w_gate': nrm((DEPTH, D_MODEL, D_FF), D_MODEL ** -0.5),
        'ffn1_w_up': nrm((DEPTH, D_MODEL, D_FF), D_MODEL ** -0.5),
        'ffn1_w_down': nrm((DEPTH, D_FF, D_MODEL), D_FF ** -0.5),
        'ffn1_post_g': gain((DEPTH, D_MODEL)),
        'mix_pre_g': gain((DEPTH, D_MODEL)),
        'w_in': nrm((DEPTH, D_MODEL, IN_COLS), D_MODEL ** -0.5),
        'ret_norm_g': gain((DEPTH, RET_HEADS, RET_DV)),
        'diff_lq1': nrm((DEPTH, DIFF_DK), 0.1),
        'diff_lk1': nrm((DEPTH, DIFF_DK), 0.1),
        'diff_lq2': nrm((DEPTH, DIFF_DK), 0.1),
        'diff_lk2': nrm((DEPTH, DIFF_DK), 0.1),
        'diff_norm_g': gain((DEPTH, DIFF_DV)),
        'w_out': nrm((DEPTH, MIX_WIDTH, D_MODEL), MIX_WIDTH ** -0.5),
        'mix_post_g': gain((DEPTH, D_MODEL)),
        'ffn2_pre_g': gain((DEPTH, D_MODEL)),
        'ffn2_w_gate': nrm((DEPTH, D_MODEL, D_FF), D_MODEL ** -0.5),
        'ffn2_w_up': nrm((DEPTH, D_MODEL, D_FF), D_MODEL ** -0.5),
        'ffn2_w_down': nrm((DEPTH, D_FF, D_MODEL), D_FF ** -0.5),
        'ffn2_post_g': gain((DEPTH, D_MODEL)),
    }


def reference(x_prompt, x_sample, state_ret, cache_diff_k, cache_diff_v,
              ffn1_pre_g, ffn1_w_gate, ffn1_w_up, ffn1_w_down, ffn1_post_g,
              mix_pre_g, w_in, ret_norm_g, diff_lq1, diff_lk1, diff_lq2, diff_lk2,
              diff_norm_g, w_out, mix_post_g,
              ffn2_pre_g, ffn2_w_gate, ffn2_w_up, ffn2_w_down, ffn2_post_g):
    xp, xs = x_prompt, x_sample
    rp, kp, vp, rs, ksm, vsm = [], [], [], [], [], []
    for li in range(DEPTH):
        lam_init = 0.8 - 0.6 * math.exp(-0.3 * li)
        w = (ffn1_pre_g[li], ffn1_w_gate[li], ffn1_w_up[li], ffn1_w_down[li], ffn1_post_g[li],
             mix_pre_g[li], w_in[li], ret_norm_g[li], diff_lq1[li], diff_lk1[li], diff_lq2[li],
             diff_lk2[li], diff_norm_g[li], w_out[li], mix_post_g[li],
             ffn2_pre_g[li], ffn2_w_gate[li], ffn2_w_up[li], ffn2_w_down[li], ffn2_post_g[li])
        s0 = jnp.zeros((xp.shape[0], RET_HEADS, RET_DK, RET_DV), jnp.float32)
        k0 = jnp.zeros((xp.shape[0], 0, DIFF_HEADS, 2, DIFF_DK), xp.dtype)
        v0 = jnp.zeros((xp.shape[0], 0, DIFF_HEADS, DIFF_DV), xp.dtype)
        xp, s_p, k_p, v_p = _layer(xp, s0, k0, v0, *w, lam_init)
        xs, s_s, k_s, v_s = _layer(xs, state_ret[li].astype(jnp.float32), cache_diff_k[li],
                                   cache_diff_v[li], *w, lam_init)
        rp.append(s_p.astype(xp.dtype)); kp.append(k_p); vp.append(v_p)
        rs.append(s_s.astype(state_ret.dtype)); ksm.append(k_s); vsm.append(v_s)
    return (xp, xs, jnp.stack(rp), jnp.stack(kp), jnp.stack(vp),
            jnp.stack(rs), jnp.stack(ksm), jnp.stack(vsm))
```

```python
import math
from contextlib import ExitStack

import numpy as np
import concourse.bass as bass
import concourse.mybir as mybir
from concourse.bass_utils import run_bass_kernel_spmd

F32 = mybir.dt.float32
BF16 = mybir.dt.bfloat16
AF = mybir.ActivationFunctionType
ALU = mybir.AluOpType
AX = mybir.AxisListType

D = 1024
DFF = 2816
NKC = 8
NFC = 22
EPS = 1e-6
LAM_INIT = 0.8 - 0.6 * math.exp(-0.3 * 0)
TABW = 272
NSLOT = 3


class Buf:
    __slots__ = ("name", "w", "r")

    def __init__(self, name):
        self.name = name
        self.w = None
        self.r = {}


class Tok:
    __slots__ = ("key", "sem", "val")

    def __init__(self, key, sem, val):
        self.key, self.sem, self.val = key, sem, val


class Eng:
    def __init__(self, name, eng, sem, is_pe=False):
        self.name, self.eng, self.sem, self.is_pe = name, eng, sem, is_pe
        self.count = 0
        self.waited = {}


class DmaQ:
    def __init__(self, name, E, sems):
        self.name, self.E, self.sems = name, E, sems
        self.uses = [0] * len(sems)
        self.next = 0


class Sched:
    def __init__(self, nc, es):
        self.nc, self.es = nc, es
        sem = lambda n: es.enter_context(nc.semaphore(n))
        self.PE = Eng("pe", nc.tensor, sem("c_pe"), is_pe=True)
        self.ACT = Eng("act", nc.scalar, sem("c_act"))
        self.DVE = Eng("dve", nc.vector, sem("c_dve"))
        self.POOL = Eng("pool", nc.gpsimd, sem("c_pool"))
        self.SP = Eng("sp", nc.sync, sem("c_sp"))
        self.qsync = DmaQ("qs", self.SP, [sem(f"qs{i}") for i in range(24)])
        self.qpool = DmaQ("qp", self.POOL, [sem(f"qp{i}") for i in range(12)])

    def _wait(self, E, tok):
        if E.waited.get(tok.key, 0) >= tok.val:
            return
        E.eng.wait_ge(tok.sem, tok.val)
        E.waited[tok.key] = tok.val

    def _deps(self, E, reads, writes):
        for b in reads:
            if b.w is not None:
                self._dep1(E, b.w)
        for b in writes:
            if b.w is not None:
                self._dep1(E, b.w)
            for t in b.r.values():
                self._dep1(E, t)

    def _dep1(self, E, tok):
        if E.is_pe and tok.key == E.name:
            return
        self._wait(E, tok)

    def _commit(self, tok, reads, writes):
        for b in writes:
            b.w = tok
            b.r = {}
        for b in reads:
            b.r[tok.key] = tok

    def op(self, E, fn, reads=(), writes=()):
        self._deps(E, reads, writes)
        ins = fn()
        E.count += 1
        ins.then_inc(E.sem, 1)
        tok = Tok(E.name, E.sem, E.count)
        self._commit(tok, reads, writes)
        return tok

    def pe(self, fn, reads=(), writes=()):
        return self.op(self.PE, fn, reads, writes)

    def act(self, fn, reads=(), writes=()):
        return self.op(self.ACT, fn, reads, writes)

    def dve(self, fn, reads=(), writes=()):
        return self.op(self.DVE, fn, reads, writes)

    def pool(self, fn, reads=(), writes=()):
        return self.op(self.POOL, fn, reads, writes)

    def dma(self, Q, out, in_, reads=(), writes=()):
        slot = Q.next % len(Q.sems)
        Q.next += 1
        sem = Q.sems[slot]
        key = (Q.name, slot)
        if Q.uses[slot] > 0:
            self._wait(Q.E, Tok(key, sem, 16 * Q.uses[slot]))
        self._deps(Q.E, reads, writes)
        Q.E.eng.dma_start(out=out, in_=in_).then_inc(sem, 16)
        Q.uses[slot] += 1
        tok = Tok(key, sem, 16 * Q.uses[slot])
        self._commit(tok, reads, writes)
        return tok

    def finish(self):
        for Q in (self.qsync, self.qpool):
            for slot, n in enumerate(Q.uses):
                if n > 0:
                    self._wait(self.SP, Tok((Q.name, slot), Q.sems[slot], 16 * n))


def build(NP, PAST):
    assert NP % 512 == 0 and PAST % 128 == 0
    NG = NP // 512
    KT_P = PAST // 128
    KMAX = max(NP, PAST + 128)
    NKT = KMAX // 128
    nc = bass.Bass("TRN2", target_bir_lowering=False)

    def din(name, shape, dt=F32):
        return nc.dram_tensor(name, list(shape), dt, kind="ExternalInput").ap()

    def dout(name, shape):
        return nc.dram_tensor(name, list(shape), F32, kind="ExternalOutput").ap()

    def dint(name, shape, dt=BF16):
        return nc.dram_tensor(name, list(shape), dt, kind="Internal").ap()

    xp = din("xp", [NP, D]); xs_in = din("xs", [64, D])
    sret = din("sret", [4, 128, 128]); ck = din("ck", [PAST, 512]); cv = din("cv", [PAST, 512])
    wg = [din("wg1", [D, DFF]), din("wg2", [D, DFF])]
    wu = [din("wu1", [D, DFF]), din("wu2", [D, DFF])]
    wd = [din("wd1", [DFF, D]), din("wd2", [DFF, D])]
    win = din("win", [D, 3584]); wout = din("wout", [D, D])
    gpre = din("gpre", [128, 3, NKC]); gpost = din("gpost", [3, 128, D])
    retg = din("retg", [128, 512]); dgin = din("dg", [128, 1]); lvec = din("lvec", [128, 256])
    tabp = din("tabp", [NP, TABW]); tabs = din("tabs", [64, TABW])
    dmask_in = din("dmask", [128, 512]); cst_in = din("cst", [128, 20]); ident_in = din("ident", [128, 128])

    yp = dout("yp", [NP, D]); ys = dout("ys", [64, D])
    retp = dout("retp", [4, 128, 128]); kp = dout("kp", [NP, 512]); vp = dout("vp", [NP, 512])
    rets = dout("rets", [4, 128, 128]); ks = dout("ks", [64, 512]); vs = dout("vs", [64, 512])

    sc_gu = [dint(f"sc_gu{f}", [11, 128, 4096]) for f in range(2)]
    sc_d = [dint(f"sc_d{f}", [11, 128, 2048]) for f in range(2)]
    sc_in = dint("sc_in", [7, 128, 4096]); sc_out = dint("sc_out", [2, 128, 4096])

    es = ExitStack()
    with es:
        S = Sched(nc, es)
        PE, ACT, DVE, POOL = S.PE, S.ACT, S.DVE, S.POOL

        def sb(name, shape, dt=F32):
            return es.enter_context(nc.sbuf_tensor("sb_" + name, list(shape), dt))

        x = sb("x", [128, 4, D]); xB = [Buf(f"x{t}") for t in range(4)]
        xs_b = [sb(f"xs_b{i}", [128, D], BF16) for i in range(2)]; xsB = [Buf(f"xs_b{i}") for i in range(2)]
        xnT = sb("xnT", [128, NKC, 512], BF16); xnB = [Buf(f"xnT{t}") for t in range(4)]
        hT = sb("hT", [128, NFC, 512], BF16); hB = [Buf(f"hT{c}") for c in range(NFC)]
        wsl = [sb(f"wsl{i}", [128, 4096], BF16) for i in range(NSLOT)]; wB = [Buf(f"wsl{i}") for i in range(NSLOT)]
        KtH = sb("KtH", [128, 4, KMAX], BF16); KtB = [Buf(f"Kt{k}") for k in range(NKT)]
        Vaug = sb("Vaug", [128, NKT, 4, 130], BF16); VB = [Buf(f"V{k}") for k in range(NKT)]
        mix = sb("mix", [128, 4, D], BF16); mixB = [Buf(f"mix{t}") for t in range(4)]
        sgr = sb("sgr", [128, 4, 512], BF16); sgrB = [Buf(f"sgr{t}") for t in range(4)]
        kdec = sb("kdec", [128, 4, 512], BF16); kdecB = [Buf(f"kdec{t}") for t in range(4)]
        gpb = sb("gpb", [128, D]); gpbB = Buf("gpb")
        retg_sb = sb("retg_sb", [128, 512]); dg_sb = sb("dg_sb", [128, 1]); ones_b = sb("ones_b", [128, 128], BF16); dmask = sb("dmask_sb", [128, 512])
        cst = sb("cst_sb", [128, 20]); ident = sb("ident", [128, 128], BF16)
        gcol = sb("gcol", [128, 3, NKC]); lam = sb("lam", [128, 8])
        constB = Buf("consts")
        tab = sb("tab", [128, 4, TABW]); tabB = Buf("tab")
        f32a = [sb(f"f32a{i}", [128, 512]) for i in range(2)]; f32aB = [Buf(f"f32a{i}") for i in range(2)]
        f32b = sb("f32b", [128, 512]); f32bB = Buf("f32b")
        f32c = sb("f32c", [128, 512]); f32cB = Buf("f32c")
        kdf = [sb(f"kdf{i}", [128, 512]) for i in range(2)]; kdfB = [Buf(f"kdf{i}") for i in range(2)]
        vdf = [sb(f"vdf{i}", [128, 512]) for i in range(1)]; vdfB = [Buf(f"vdf{i}") for i in range(1)]
        rt = sb("rt", [128, 4, 64]); rtB = Buf("rt")
        tb = [sb(f"tb{i}", [128, 2, 512], BF16) for i in range(3)]; tbB = [Buf(f"tb{i}") for i in range(3)]
        pT = [sb(f"pT{i}", [128, 512], BF16) for i in range(3)]; pTB = [Buf(f"pT{i}") for i in range(3)]
        sTm = [sb(f"sTm{i}", [128, 128], BF16) for i in range(4)]; sTmB = [Buf(f"sTm{i}") for i in range(4)]
        sg = [sb(f"sg{i}", [128, 512], BF16) for i in range(2)]; sgB = [Buf(f"sg{i}") for i in range(2)]
        ptmp = sb("ptmp", [128, D]); ptmpB = [Buf("ptmp0"), Buf("ptmp1")]
        lv = ptmp[:, 0:256]; identf = ptmp[:, 256:384]
        pTd = [sb(f"pTd{i}", [128, 512], BF16) for i in range(3)]; pTdB = [Buf(f"pTd{i}") for i in range(3)]
        Sst = sb("Sst", [128, 4, 128]); SstB = Buf("Sst")
        Sb = sb("Sb", [128, 4, 128], BF16); SbB = Buf("Sb")
        st = sb("st", [128, 64]); stB = Buf("st")
        mdB = [Buf(f"md{h}") for h in range(4)]
        qrT = hT[:, 0:4, :]; qdecT = hT[:, 4:8, :]; krT = hT[:, 8:12, :]; QdT = hT[:, 12:16, :]; vr = hT[:, 16:20, :]
        qrTB, qdecTB, krTB, QdTB, vrB = hB[0:4], hB[4:8], hB[8:12], hB[12:16], hB[16:20]
        mixT, mixTB = xnT, xnB

        psum_all = es.enter_context(nc.psum_tensor("psum_all", [128, 4096], F32))
        banks = [psum_all[:, i * 512:(i + 1) * 512] for i in range(8)]
        bankB = [Buf(f"bank{i}") for i in range(8)]
        psum16 = psum_all[:].bitcast(BF16)
        bank16 = [psum16[:, i * 1024:(i + 1) * 1024] for i in range(8)]

        outB = Buf("outputs")

        q = S.qsync
        ctoks = []
        for dst, src in ((retg_sb, retg), (dg_sb, dgin), (dmask, dmask_in), (cst, cst_in), (identf, ident_in), (lv, lvec)):
            ctoks.append(S.dma(q, dst[:] if dst not in (identf, lv) else dst, src))
        ctoks.append(S.dma(q, gcol[:], gpre))
        for E_ in (PE, ACT, DVE, POOL):
            for tk in ctoks:
                S._wait(E_, tk)
        S.dve(lambda: nc.vector.tensor_copy(out=ident[:], in_=identf), reads=[constB], writes=[constB, ptmpB[0]])
        for i_ in range(3):
            S.dve(lambda i_=i_: nc.vector.memset(pTd[i_][:], 0.0), writes=[pTdB[i_]])
        S.dve(lambda: nc.vector.memset(ones_b[:], 1.0), reads=[constB], writes=[constB])
        S.dve(lambda: nc.vector.tensor_scalar(out=dg_sb[:], in0=dg_sb[:], scalar1=1.0 - LAM_INIT, scalar2=None, op0=ALU.mult),
              reads=[constB], writes=[constB])
        S.dve(lambda: nc.vector.tensor_tensor(out=lv[:, 0:64], in0=lv[:, 0:64], in1=lv[:, 64:128], op=ALU.mult), reads=[constB], writes=[constB, ptmpB[0]])
        S.dve(lambda: nc.vector.tensor_tensor(out=lv[:, 128:192], in0=lv[:, 128:192], in1=lv[:, 192:256], op=ALU.mult), reads=[constB], writes=[constB, ptmpB[0]])
        S.dve(lambda: nc.vector.tensor_reduce(out=lam[:, 0:1], in_=lv[:, 0:64], axis=AX.X, op=ALU.add), reads=[constB], writes=[constB, ptmpB[0]])
        S.dve(lambda: nc.vector.tensor_reduce(out=lam[:, 1:2], in_=lv[:, 128:192], axis=AX.X, op=ALU.add), reads=[constB], writes=[constB, ptmpB[0]])
        S.act(lambda: nc.scalar.activation(out=lam[:, 4:6], in_=lam[:, 0:2], func=AF.Exp), reads=[constB], writes=[constB])
        S.dve(lambda: nc.vector.tensor_tensor(out=lam[:, 2:3], in0=lam[:, 5:6], in1=lam[:, 4:5], op=ALU.subtract), reads=[constB], writes=[constB])
        S.dve(lambda: nc.vector.tensor_scalar(out=lam[:, 3:4], in0=lam[:, 2:3], scalar1=-LAM_INIT, scalar2=None, op0=ALU.add), reads=[constB], writes=[constB])
        nlam = lam[:, 3:4]
        epsc = lam[:, 6:8]
        S.dve(lambda: nc.vector.memset(lam[:, 6:7], EPS), reads=[constB], writes=[constB])
        S.dve(lambda: nc.vector.memset(lam[:, 7:8], 4.0 * EPS), reads=[constB], writes=[constB])

        def gu_unit(f, u):
            srcs = [(lambda sl, j=j: sl[:, j * 2048:(j + 1) * 2048].rearrange("p (kc c) -> p kc c", kc=8),
                     w_[f][:, u * 256:(u + 1) * 256].rearrange("(kc p) c -> p kc c", p=128)) for j, w_ in enumerate((wg, wu))]
            return (sc_gu[f][u], 4096, srcs)

        def d_unit(f, v):
            srcs = [(lambda sl: sl[:, 0:2048].rearrange("p (kk c) -> p kk c", kk=2),
                     wd[f][v * 256:(v + 1) * 256, :].rearrange("(kk p) c -> p kk c", p=128))]
            return (sc_d[f][v], 2048, srcs)

        def in_unit(b):
            srcs = [(lambda sl: sl[:, 0:4096].rearrange("p (kc c) -> p kc c", kc=8),
                     win[:, b * 512:(b + 1) * 512].rearrange("(kc p) c -> p kc c", p=128))]
            return (sc_in[b], 4096, srcs)

        def out_unit(o):
            srcs = [(lambda sl: sl[:, 0:4096].rearrange("p (kk c) -> p kk c", kk=4),
                     wout[o * 512:(o + 1) * 512, :].rearrange("(kk p) c -> p kk c", p=128))]
            return (sc_out[o], 4096, srcs)

        group_units = ([gu_unit(0, u) for u in range(11)] + [d_unit(0, v) for v in range(11)]
                       + [in_unit(b) for b in (0, 2, 1, 3, 5, 6, 4)] + [out_unit(o) for o in range(2)]
                       + [gu_unit(1, u) for u in range(11)] + [d_unit(1, v) for v in range(11)])
        NU = len(group_units)
        scB = [Buf(f"sc{i}") for i in range(NU)]
        n_groups_total = NG + 1
        wstate = {"loaded": 0}

        def emit_load(gi):
            g, ui = divmod(gi, NU)
            sc_ap, n, srcs = group_units[ui]
            slot = gi % NSLOT
            if g == 0:
                for viewfn, src in srcs:
                    S.dma(S.qpool, viewfn(wsl[slot]), src, writes=[wB[slot]])
                S.dma(S.qsync, sc_ap, wsl[slot][:, 0:n], reads=[wB[slot]], writes=[scB[ui]])
            else:
                S.dma(S.qsync, wsl[slot][:, 0:n], sc_ap, reads=[scB[ui]], writes=[wB[slot]])

        def get_unit(gi):
            lim = min(gi + NSLOT, n_groups_total * NU)
            while wstate["loaded"] < lim:
                emit_load(wstate["loaded"])
                wstate["loaded"] += 1
            return gi % NSLOT

        cnt = {"bank": 0, "sbank": 0, "tb": 0, "pT": 0, "pTd": 0, "sTm": 0, "sg": 0, "f32a": 0, "kdf": 0, "vdf": 0}

        def rr(name, n):
            v = cnt[name] % n
            cnt[name] += 1
            return v

        def rstd_from_sumsq(ss_ap, out_ap, n_feat, rows, cols_buf):
            S.dve(lambda: nc.vector.tensor_scalar(out=out_ap, in0=ss_ap, scalar1=1.0 / n_feat, scalar2=EPS, op0=ALU.mult, op1=ALU.add),
                  reads=[stB], writes=[stB])
            S.act(lambda: nc.scalar.activation(out=out_ap, in_=out_ap, func=AF.Sqrt), reads=[stB], writes=[stB])
            S.dve(lambda: nc.vector.reciprocal(out=out_ap, in_=out_ap), reads=[stB], writes=[stB])

        stQ_g = Buf("stQ")
        stT = [Buf(f"stT{t}") for t in range(4)]
        stP = [Buf(f"stP{t}") for t in range(4)]

        def prenorm_T(ntiles, nt, gi_):
            def stage1(t):
                xi = t % 2
                S.act(lambda: nc.scalar.activation(out=xs_b[xi][:nt, :], in_=x[:nt, t, :], func=AF.Square, accum_out=st[:nt, t:t + 1]),
                      reads=[xB[t], stT[t]], writes=[xsB[xi], stT[t]])
                S.act(lambda: nc.scalar.activation(out=st[:nt, 4 + t:5 + t], in_=st[:nt, t:t + 1], func=AF.Sqrt, scale=1.0 / D, bias=epsc[:nt, 0:1]),
                      reads=[stT[t], constB], writes=[stT[t]])
                S.dve(lambda: nc.vector.reciprocal(out=st[:nt, 4 + t:5 + t], in_=st[:nt, 4 + t:5 + t]), reads=[stT[t]], writes=[stT[t]])
                S.dve(lambda: nc.vector.tensor_scalar(out=xs_b[xi][:nt, :], in0=x[:nt, t, :], scalar1=st[:nt, 4 + t:5 + t], scalar2=None, op0=ALU.mult),
                      reads=[xB[t], stT[t]], writes=[xsB[xi]])

            def stage2(t):
                xi = t % 2
                bk = rr("bank", 8)
                for kc in range(NKC):
                    S.pe(lambda kc=kc: nc.tensor.transpose(bank16[bk][:, kc * 128:kc * 128 + nt], xs_b[xi][:nt, kc * 128:(kc + 1) * 128], ident[:nt, :nt]),
                         reads=[xsB[xi], constB], writes=[bankB[bk]])
                S.dve(lambda: nc.vector.tensor_tensor(
                    out=xnT[:, :, t * 128:t * 128 + nt],
                    in0=bank16[bk].rearrange("p (kc c) -> p kc c", kc=8)[:, :, 0:nt],
                    in1=gcol[:, gi_, :].unsqueeze(2).to_broadcast([128, NKC, nt]), op=ALU.mult),
                    reads=[bankB[bk], constB], writes=[xnB[t]] + mdB)

            for t in range(ntiles + 1):
                if t < ntiles:
                    stage1(t)
                if t >= 1:
                    stage2(t - 1)

        def postnorm_residual(ntiles, nt, gidx, half):
            a = 1.0 / (half * half)
            ecol = 0 if half == 1.0 else 1
            for t in range(ntiles):
                xi = t % 2
                yv = psum_all[:nt, 2 * t * 512:(2 * t + 2) * 512]
                yB = [bankB[2 * t], bankB[2 * t + 1]]
                S.act(lambda t=t, xi=xi, yv=yv: nc.scalar.activation(out=xs_b[xi][:nt, :], in_=yv, func=AF.Square, accum_out=st[:nt, 8 + t:9 + t]),
                      reads=yB + [stP[t]], writes=[xsB[xi], stP[t]])
                S.act(lambda t=t: nc.scalar.activation(out=st[:nt, 16 + t:17 + t], in_=st[:nt, 8 + t:9 + t], func=AF.Sqrt, scale=a / D, bias=epsc[:nt, ecol:ecol + 1]),
                      reads=[stP[t], constB], writes=[stP[t]])
                S.dve(lambda t=t: nc.vector.reciprocal(out=st[:nt, 20 + t:21 + t], in_=st[:nt, 16 + t:17 + t]), reads=[stP[t]], writes=[stP[t]])
                for hh in range(2):
                    bk = 2 * t + hh
                    S.dve(lambda t=t, hh=hh, bk=bk: nc.vector.scalar_tensor_tensor(
                        out=ptmp[:nt, hh * 512:(hh + 1) * 512], in0=banks[bk][:nt, :], scalar=st[:nt, 20 + t:21 + t],
                        in1=gpb[:nt, hh * 512:(hh + 1) * 512], op0=ALU.mult, op1=ALU.mult),
                        reads=[bankB[bk], stP[t], gpbB], writes=[ptmpB[hh]])
                    if (2 * t + hh) % 3 == 2:
                        S.dve(lambda t=t, hh=hh: nc.vector.tensor_tensor(out=x[:nt, t, hh * 512:(hh + 1) * 512], in0=ptmp[:nt, hh * 512:(hh + 1) * 512],
                                                                         in1=x[:nt, t, hh * 512:(hh + 1) * 512], op=ALU.add),
                              reads=[ptmpB[hh], xB[t]], writes=[xB[t]])
                    else:
                        S.pool(lambda t=t, hh=hh: nc.gpsimd.tensor_tensor(out=x[:nt, t, hh * 512:(hh + 1) * 512], in0=ptmp[:nt, hh * 512:(hh + 1) * 512],
                                                                          in1=x[:nt, t, hh * 512:(hh + 1) * 512], op=ALU.add),
                               reads=[ptmpB[hh], xB[t]], writes=[xB[t]])

        def ffn(ntiles, nt, ubase, gidx, inter=None):
            ntok = (ntiles - 1) * 128 + nt
            S.dma(S.qsync, gpb[:], gpost[gidx], writes=[gpbB])
            prenorm_T(ntiles, nt, gidx)
            for u in range(11):
                slot = get_unit(ubase + u)
                wv = wsl[slot][:, :].rearrange("p (j kc c) -> p j kc c", j=2, kc=8)
                for cc in range(2):
                    c = 2 * u + cc
                    bg, bu = rr("bank", 8), rr("bank", 8)
                    for j, bk in ((0, bg), (1, bu)):
                        for kc in range(NKC):
                            S.pe(lambda j=j, bk=bk, kc=kc, cc=cc, wv=wv: nc.tensor.matmul(
                                banks[bk][:, 0:ntok], wv[:, j, kc, cc * 128:(cc + 1) * 128], xnT[:, kc, 0:ntok],
                                start=(kc == 0), stop=(kc == NKC - 1)),
                                reads=[wB[slot]] + xnB[0:ntiles] + mdB, writes=[bankB[bk]])
                    si = rr("sg", 2)
                    S.act(lambda bg=bg, si=si: nc.scalar.activation(out=sg[si][:, 0:ntok], in_=banks[bg][:, 0:ntok], func=AF.Silu),
                          reads=[bankB[bg]], writes=[sgB[si]])
                    S.dve(lambda bu=bu, si=si, c=c: nc.vector.tensor_tensor(out=hT[:, c, 0:ntok], in0=banks[bu][:, 0:ntok], in1=sg[si][:, 0:ntok], op=ALU.mult),
                          reads=[bankB[bu], sgB[si]], writes=[hB[c]])
                if inter:
                    inter.pop(0)()
            while inter:
                inter.pop(0)()
            def down_mm(slot, kk, kc, t, hh):
                wv = wsl[slot][:, 0:2048].rearrange("p (kk c) -> p kk c", kk=2)
                bk = 2 * t + hh
                S.pe(lambda: nc.tensor.matmul(banks[bk][:nt, :], hT[:, kc, t * 128:t * 128 + nt], wv[:, kk, hh * 512:(hh + 1) * 512],
                                              start=(kc == 0), stop=(kc == NFC - 1)),
                     reads=[wB[slot], hB[kc]], writes=[bankB[bk]])

            NTAIL = 3 if ntiles > 1 else 0
            for v in range(11 - NTAIL):
                slot = get_unit(ubase + 11 + v)
                for kk in range(2):
                    for t in range(ntiles):
                        for hh in range(2):
                            down_mm(slot, kk, 2 * v + kk, t, hh)
            if NTAIL:
                g0 = ubase + 11 + (11 - NTAIL)
                get_unit(g0)
                for t in range(ntiles):
                    for v in range(11 - NTAIL, 11):
                        slot = (ubase + 11 + v) % NSLOT
                        for kk in range(2):
                            for hh in range(2):
                                down_mm(slot, kk, 2 * v + kk, t, hh)
            postnorm_residual(ntiles, nt, gidx, 0.5)

        def transposes(srcs, nt, evac="dve"):
            for src, k, sB, dst, dB in srcs:
                bk = rr("bank", 8)
                for i in range(k):
                    S.pe(lambda i=i, bk=bk, src=src: nc.tensor.transpose(bank16[bk][:, i * 128:i * 128 + nt], src[:nt, i * 128:(i + 1) * 128], ident[:nt, :nt]),
                         reads=[sB, constB], writes=[bankB[bk]])
                if evac == "act":
                    S.act(lambda bk=bk, k=k, dst=dst: nc.scalar.copy(out=dst, in_=bank16[bk][:, 0:k * 128].rearrange("p (k c) -> p k c", k=k)[:, :, 0:nt]),
                          reads=[bankB[bk]], writes=dB)
                else:
                    S.dve(lambda bk=bk, k=k, dst=dst: nc.vector.tensor_copy(out=dst, in_=bank16[bk][:, 0:k * 128].rearrange("p (k c) -> p k c", k=k)[:, :, 0:nt]),
                          reads=[bankB[bk]], writes=dB)

        def rope_full(src, srcB, nt, t, coff, outs):
            V8 = lambda a: a[:nt, :].rearrange("p (m d) -> p m d", d=64)
            V42 = lambda a: a[:nt, :].rearrange("p (h two d) -> p h two d", h=4, two=2)
            cosb8 = tab[:nt, t, coff:coff + 64].unsqueeze(1).to_broadcast([nt, 8, 64])
            sinb4 = tab[:nt, t, coff + 64:coff + 128].unsqueeze(1).to_broadcast([nt, 4, 64])
            S.dve(lambda: nc.vector.tensor_tensor(out=V8(f32b), in0=V8(src), in1=cosb8, op=ALU.mult), reads=[srcB, tabB], writes=[f32bB])
            S.pool(lambda: nc.gpsimd.tensor_tensor(out=V42(f32c)[:, :, 0, :], in0=V42(src)[:, :, 1, :], in1=sinb4, op=ALU.mult), reads=[srcB, tabB], writes=[f32cB])
            S.pool(lambda: nc.gpsimd.tensor_tensor(out=V42(f32c)[:, :, 1, :], in0=V42(src)[:, :, 0, :], in1=sinb4, op=ALU.mult), reads=[srcB, tabB], writes=[f32cB])
            S.dve(lambda: nc.vector.tensor_tensor(out=V42(src)[:, :, 0, :], in0=V42(f32b)[:, :, 0, :], in1=V42(f32c)[:, :, 0, :], op=ALU.subtract),
                  reads=[f32bB, f32cB], writes=[srcB])
            S.dve(lambda: nc.vector.tensor_tensor(out=V42(src)[:, :, 1, :], in0=V42(f32b)[:, :, 1, :], in1=V42(f32c)[:, :, 1, :], op=ALU.add),
                  reads=[f32bB, f32cB], writes=[srcB])

        def rope_partial(src, srcB, nt, t):
            V = src[:nt, :].rearrange("p (m d) -> p m d", d=64)
            X1, X2 = V[:, :, 0:8], V[:, :, 8:16]
            cosb = tab[:nt, t, 256:264].unsqueeze(1).to_broadcast([nt, 8, 8])
            sinb = tab[:nt, t, 264:272].unsqueeze(1).to_broadcast([nt, 8, 8])
            R = [rt[:nt, i, :].rearrange("p (m d) -> p m d", d=8) for i in range(4)]
            S.dve(lambda: nc.vector.tensor_tensor(out=R[0], in0=X1, in1=cosb, op=ALU.mult), reads=[srcB, tabB], writes=[rtB])
            S.dve(lambda: nc.vector.tensor_tensor(out=R[1], in0=X2, in1=sinb, op=ALU.mult), reads=[srcB, tabB], writes=[rtB])
            S.dve(lambda: nc.vector.tensor_tensor(out=R[2], in0=X2, in1=cosb, op=ALU.mult), reads=[srcB, tabB], writes=[rtB])
            S.dve(lambda: nc.vector.tensor_tensor(out=R[3], in0=X1, in1=sinb, op=ALU.mult), reads=[srcB, tabB], writes=[rtB])
            S.dve(lambda: nc.vector.tensor_tensor(out=X1, in0=R[0], in1=R[1], op=ALU.subtract), reads=[rtB], writes=[srcB])
            S.dve(lambda: nc.vector.tensor_tensor(out=X2, in0=R[2], in1=R[3], op=ALU.add), reads=[rtB], writes=[srcB])

        def mixer(ntiles, nt, ubase, kbase, is_sample, kout, vout, retout):
            ntok = (ntiles - 1) * 128 + nt
            S.dma(S.qsync, gpb[:], gpost[1], writes=[gpbB])
            prenorm_T(ntiles, nt, 1)
            qdec_c = cst[:, 0:4]
            kdec_c = cst[:, 8:12] if is_sample else cst[:, 4:8]
            blk_c = cst[:, 16:20] if is_sample else cst[:, 12:16]
            pending = []
            stepc = {"i": 0}

            def flush(all_=False):
                while pending and (all_ or pending[0][0] <= stepc["i"] - 3):
                    pending.pop(0)[1]()

            Qp = (hT[:, 12:16, :], xnT[:, 0:4, :])

            def q_transposes(T_, TB_, t):
                tsl_ = slice(t * 128, t * 128 + nt)
                bk = rr("bank", 8)
                for i in range(4):
                    S.pe(lambda i=i: nc.tensor.transpose(bank16[bk][:, i * 128:i * 128 + nt], T_[:nt, 0, i * 128:(i + 1) * 128], ident[:nt, :nt]),
                         reads=[TB_, constB], writes=[bankB[bk]])
                src = bank16[bk][:, 0:512].rearrange("p (k c) -> p k c", k=4)
                S.pool(lambda: nc.gpsimd.memset(Qp[0][64:128, :, tsl_], 0.0), writes=QdTB)
                S.pool(lambda: nc.gpsimd.memset(Qp[1][0:64, :, tsl_], 0.0), writes=[xnB[t]])
                S.act(lambda: nc.scalar.copy(out=Qp[0][0:64, :, tsl_], in_=src[0:64, :, 0:nt]), reads=[bankB[bk]], writes=QdTB)
                S.act(lambda: nc.scalar.copy(out=Qp[1][64:128, :, tsl_], in_=src[64:128, :, 0:nt]), reads=[bankB[bk]], writes=[xnB[t]])

            ret_state = {}
            stQ = stQ_g

            def ret_P1(t):
                bs, bkv = rr("bank", 8), rr("bank", 8)
                tsl = slice(t * 128, t * 128 + nt)
                for h in range(4):
                    S.pe(lambda h=h: nc.tensor.matmul(banks[bs][:nt, h * 128:h * 128 + nt], krT[:, h, tsl], qrT[:, h, tsl], start=True, stop=True),
                         reads=[krTB[h], qrTB[h]], writes=[bankB[bs]])
                for h in range(4):
                    S.pe(lambda h=h: nc.tensor.matmul(banks[bkv][:, h * 128:(h + 1) * 128], kdec[:nt, t, h * 128:(h + 1) * 128], vr[:nt, t, h * 128:(h + 1) * 128],
                                                      start=True, stop=True),
                         reads=[kdecB[t], vrB[t]], writes=[bankB[bkv]])
                for h in range(4):
                    S.dve(lambda h=h: nc.vector.tensor_tensor(out=sTm[h][:nt, :nt], in0=banks[bs][:nt, h * 128:h * 128 + nt], in1=dmask[:nt, h * 128:h * 128 + nt], op=ALU.mult),
                          reads=[bankB[bs], constB], writes=[sTmB[h]])
                ret_state[t] = bkv

            def ret_P2(t):
                bkv = ret_state.pop(t)
                bo = rr("bank", 8)
                tsl = slice(t * 128, t * 128 + nt)
                for h in range(4):
                    S.pe(lambda h=h: nc.tensor.matmul(banks[bo][:nt, h * 128:(h + 1) * 128], sTm[h][:nt, :nt], vr[:nt, t, h * 128:(h + 1) * 128], start=True, stop=False),
                         reads=[sTmB[h], vrB[t]], writes=[bankB[bo]])
                    S.pe(lambda h=h: nc.tensor.matmul(banks[bo][:nt, h * 128:(h + 1) * 128], qdecT[:, h, tsl], Sb[:, h, :], start=False, stop=True),
                         reads=[qdecTB[h], SbB], writes=[bankB[bo]])
                for h in range(4):
                    S.dve(lambda h=h: nc.vector.scalar_tensor_tensor(out=Sst[:, h, :], in0=Sst[:, h, :], scalar=blk_c[:, h:h + 1], in1=banks[bkv][:, h * 128:(h + 1) * 128],
                                                                     op0=ALU.mult, op1=ALU.add),
                          reads=[SstB, bankB[bkv], constB], writes=[SstB])
                S.act(lambda: nc.scalar.copy(out=Sb[:, :, :], in_=Sst[:, :, :]), reads=[SstB], writes=[SbB])
                RO, ROB = f32b, f32bB
                S.act(lambda: nc.scalar.copy(out=RO[:nt, :], in_=banks[bo][:nt, :]), reads=[bankB[bo]], writes=[ROB])

            def ret_B(t):
                RO, ROB = f32b, f32bB
                V4 = lambda a_: a_[:nt, :].rearrange("p (h d) -> p h d", h=4)
                S.dve(lambda: nc.vector.tensor_reduce(out=st[:nt, 24:28], in_=V4(RO), axis=AX.X, op=ALU.add), reads=[ROB], writes=[stB])
                for h in range(4):
                    S.act(lambda h=h: nc.scalar.activation(out=f32c[:nt, h * 128:(h + 1) * 128], in_=RO[:nt, h * 128:(h + 1) * 128], func=AF.Square,
                                                           accum_out=st[:nt, 48 + h:49 + h]),
                          reads=[ROB, stQ], writes=[f32cB, stQ])
                S.dve(lambda: nc.vector.tensor_copy(out=st[:nt, 28:32], in_=st[:nt, 48:52]), reads=[stQ, stB], writes=[stB])
                S.dve(lambda: nc.vector.tensor_scalar(out=st[:nt, 24:28], in0=st[:nt, 24:28], scalar1=1.0 / 128, scalar2=None, op0=ALU.mult), reads=[stB], writes=[stB])
                S.dve(lambda: nc.vector.tensor_tensor(out=st[:nt, 32:36], in0=st[:nt, 24:28], in1=st[:nt, 24:28], op=ALU.mult), reads=[stB], writes=[stB])
                S.dve(lambda: nc.vector.scalar_tensor_tensor(out=st[:nt, 28:32], in0=st[:nt, 28:32], scalar=1.0 / 128, in1=st[:nt, 32:36], op0=ALU.mult, op1=ALU.subtract),
                      reads=[stB], writes=[stB])
                S.act(lambda: nc.scalar.activation(out=st[:nt, 28:32], in_=st[:nt, 28:32], func=AF.Sqrt, scale=1.0, bias=epsc[:nt, 0:1]), reads=[stB, constB], writes=[stB])
                S.dve(lambda: nc.vector.reciprocal(out=st[:nt, 28:32], in_=st[:nt, 28:32]), reads=[stB], writes=[stB])
                S.dve(lambda: nc.vector.scalar_tensor_tensor(out=st[:nt, 32:36], in0=st[:nt, 24:28], scalar=-1.0, in1=st[:nt, 28:32], op0=ALU.mult, op1=ALU.mult),
                      reads=[stB], writes=[stB])
                for h in range(4):
                    S.dve(lambda h=h: nc.vector.tensor_scalar(out=f32c[:nt, h * 128:(h + 1) * 128], in0=RO[:nt, h * 128:(h + 1) * 128],
                                                              scalar1=st[:nt, 28 + h:29 + h], scalar2=st[:nt, 32 + h:33 + h], op0=ALU.mult, op1=ALU.add),
                          reads=[ROB, stB], writes=[f32cB])
                S.pool(lambda: nc.gpsimd.tensor_tensor(out=f32c[:nt, :], in0=f32c[:nt, :], in1=retg_sb[:nt, :], op=ALU.mult), reads=[f32cB, constB], writes=[f32cB])
                S.pool(lambda: nc.gpsimd.tensor_tensor(out=mix[:nt, t, 0:512], in0=f32c[:nt, :], in1=sgr[:nt, t, :], op=ALU.mult), reads=[f32cB, sgrB[t]], writes=[mixB[t]])

            ret_queue = []
            for t_ in range(ntiles):
                if t_ == 0:
                    ret_queue.append(lambda: ret_P1(0))
                else:
                    ret_queue.append(lambda t_=t_: (ret_P1(t_), ret_B(t_ - 1)))
                ret_queue.append(lambda t_=t_: ret_P2(t_))
            ret_queue.append(lambda: ret_B(ntiles - 1))

            for bpos in range(7):
                slot = get_unit(ubase + bpos)
                wv = wsl[slot][:, :].rearrange("p (kc c) -> p kc c", kc=8)
                b = (0, 2, 1, 3, 5, 6, 4)[bpos]
                for t in range(ntiles):
                    bk = rr("bank", 8)
                    for kc in range(NKC):
                        S.pe(lambda kc=kc, t=t, bk=bk, wv=wv: nc.tensor.matmul(
                            banks[bk][:nt, :], xnT[:, kc, t * 128:t * 128 + nt], wv[:, kc, :], start=(kc == 0), stop=(kc == NKC - 1)),
                            reads=[wB[slot], xnB[t]] + mdB, writes=[bankB[bk]])
                    stepc["i"] += 1
                    flush()
                    if bpos >= 4 and ret_queue:
                        if bpos == 4 and t == 0:
                            flush(True)
                        ret_queue.pop(0)()
                    kt = kbase + t
                    if b in (0, 1):
                        ai = rr("f32a", 2); A, AB = f32a[ai], f32aB[ai]
                        S.act(lambda bk=bk, A=A: nc.scalar.copy(out=A[:nt, :], in_=banks[bk][:nt, :]), reads=[bankB[bk]], writes=[AB])
                        rope_full(A, AB, nt, t, 0 if b == 0 else 128, None)
                        ti = rr("tb", 3); T_, TB_ = tb[ti], tbB[ti]
                        S.act(lambda A=A, T_=T_: nc.scalar.copy(out=T_[:nt, 0, :], in_=A[:nt, :]), reads=[AB], writes=[TB_])
                        if b == 0:
                            S.pool(lambda A=A, T_=T_: nc.gpsimd.tensor_tensor(
                                out=T_[:nt, 1, :].rearrange("p (h d) -> p h d", h=4), in0=A[:nt, :].rearrange("p (h d) -> p h d", h=4),
                                in1=qdec_c[:nt, :].unsqueeze(2).to_broadcast([nt, 4, 128]), op=ALU.mult), reads=[AB, constB], writes=[TB_])
                            pending.append((stepc["i"], lambda T_=T_, TB_=TB_, t=t: transposes(
                                [(T_[:, 0, :], 4, TB_, qrT[:, :, t * 128:t * 128 + nt], [qrTB[h] for h in range(4)]),
                                 (T_[:, 1, :], 4, TB_, qdecT[:, :, t * 128:t * 128 + nt], [qdecTB[h] for h in range(4)])], nt, "act")))
                        else:
                            S.pool(lambda A=A, t=t: nc.gpsimd.tensor_tensor(
                                out=kdec[:nt, t, :].rearrange("p (h d) -> p h d", h=4), in0=A[:nt, :].rearrange("p (h d) -> p h d", h=4),
                                in1=kdec_c[:nt, :].unsqueeze(2).to_broadcast([nt, 4, 128]), op=ALU.mult), reads=[AB, constB], writes=[kdecB[t]])
                            pending.append((stepc["i"], lambda T_=T_, TB_=TB_, t=t: transposes(
                                [(T_[:, 0, :], 4, TB_, krT[:, :, t * 128:t * 128 + nt], [krTB[h] for h in range(4)])], nt, "act")))
                    elif b == 2:
                        S.act(lambda bk=bk, t=t: nc.scalar.copy(out=vr[:nt, t, :], in_=banks[bk][:nt, :]), reads=[bankB[bk]], writes=[vrB[t]])
                    elif b == 3:
                        S.act(lambda bk=bk, t=t: nc.scalar.activation(out=sgr[:nt, t, :], in_=banks[bk][:nt, :], func=AF.Silu), reads=[bankB[bk]], writes=[sgrB[t]])
                    elif b in (4, 5):
                        if b == 4:
                            ai = rr("f32a", 2); A, AB = f32a[ai], f32aB[ai]
                        else:
                            ai = rr("kdf", 2); A, AB = kdf[ai], kdfB[ai]
                        S.act(lambda bk=bk, A=A: nc.scalar.copy(out=A[:nt, :], in_=banks[bk][:nt, :]), reads=[bankB[bk]], writes=[AB])
                        rope_partial(A, AB, nt, t)
                        ti = rr("tb", 3); T_, TB_ = tb[ti], tbB[ti]
                        S.act(lambda A=A, T_=T_: nc.scalar.copy(out=T_[:nt, 0, :], in_=A[:nt, :]), reads=[AB], writes=[TB_])
                        if b == 4:
                            pending.append((stepc["i"], lambda T_=T_, TB_=TB_, t=t: q_transposes(T_, TB_, t)))
                        else:
                            S.dma(S.qsync, kout[t * 128:t * 128 + nt, :], A[:nt, :], reads=[AB], writes=[])
                            pending.append((stepc["i"], lambda T_=T_, TB_=TB_, kt=kt: transposes(
                                [(T_[:, 0, :], 4, TB_, KtH[:, :, kt * 128:kt * 128 + nt], [KtB[kt]])], nt, "act")))
                    else:
                        ai = 0; A, AB = vdf[ai], vdfB[ai]
                        S.act(lambda bk=bk, A=A: nc.scalar.copy(out=A[:nt, :], in_=banks[bk][:nt, :]), reads=[bankB[bk]], writes=[AB])
                        S.dma(S.qsync, vout[t * 128:t * 128 + nt, :], A[:nt, :], reads=[AB], writes=[])
                        S.dve(lambda A=A, kt=kt: nc.vector.tensor_copy(out=Vaug[:nt, kt, :, 0:128], in_=A[:nt, :].rearrange("p (h d) -> p h d", h=4)),
                              reads=[AB], writes=[VB[kt]])
            flush(True)

            while ret_queue:
                ret_queue.pop(0)()
            if retout is not None:
                S.dma(S.qsync, retout.rearrange("h d e -> d h e"), Sst[:, :, :], reads=[SstB], writes=[])

            nkt = kbase + ntiles
            W = ntok
            OTb = (4, 5)
            SMb = (6, 7)
            for h in range(4):
                if is_sample:
                    kts_list = [list(range(k0, min(k0 + 8, kbase))) for k0 in range(0, kbase, 8)] + [[kbase]]
                else:
                    kts_list = [[kt] for kt in range(nkt)]
                steps = [(kts, c) for kts in kts_list for c in range(2)]
                info = {}
                nmm = {0: 0, 1: 0}
                for kts, c in steps:
                    nmm[c] += len(kts)
                cntc = {0: 0, 1: 0}

                def emit_st(kts, c):
                    kt0 = kts[0]
                    lt = kt0 - kbase
                    nk = nt if (lt == ntiles - 1) else 128
                    qlo = 0 if (is_sample or lt < 0) else lt
                    ncol = ntok - qlo * 128
                    diag = lt >= 0 and not is_sample
                    bk = rr("sbank", 4)
                    qB = [QdTB[h]] if c == 0 else xnB[qlo:ntiles]
                    for j, kt in enumerate(kts):
                        S.pe(lambda j=j, kt=kt: nc.tensor.matmul(banks[bk][:nk, j * ncol:(j + 1) * ncol], KtH[:, h, kt * 128:kt * 128 + nk],
                                                                 Qp[c][:, h, qlo * 128:ntok], start=True, stop=True),
                             reads=[KtB[kt]] + qB, writes=[bankB[bk]])
                    wtot = ncol * len(kts)
                    if diag:
                        pi = rr("pTd", 3)
                        P_, PB_ = pTd[pi], pTdB[pi]
                        S.act(lambda: nc.scalar.activation(out=P_[:, 64:ncol], in_=banks[bk][:, 64:ncol], func=AF.Exp, scale=0.125),
                              reads=[bankB[bk]], writes=[PB_])
                        S.act(lambda: nc.scalar.activation(out=P_[0:64, 0:64], in_=banks[bk][0:64, 0:64], func=AF.Exp, scale=0.125),
                              reads=[bankB[bk]], writes=[PB_])
                    else:
                        pi = rr("pT", 3)
                        P_, PB_ = pT[pi], pTB[pi]
                        S.act(lambda: nc.scalar.activation(out=P_[:nk, 0:wtot], in_=banks[bk][:nk, 0:wtot], func=AF.Exp, scale=0.125),
                              reads=[bankB[bk]], writes=[PB_])
                    info[(kts[0], c)] = (nk, qlo, ncol, P_, PB_)

                def emit_pv(kts, c):
                    nk, qlo, ncol, P_, PB_ = info.pop((kts[0], c))
                    for j, kt in enumerate(kts):
                        first = cntc[c] == 0
                        lastf = cntc[c] == nmm[c] - 1
                        cntc[c] += 1
                        S.pe(lambda j=j, kt=kt, first=first, lastf=lastf: nc.tensor.matmul(
                            banks[OTb[c]][:, qlo * 128:ntok], Vaug[:nk, kt, h, 0:128], P_[:nk, j * ncol:(j + 1) * ncol], start=first, stop=lastf),
                            reads=[PB_, VB[kt]], writes=[bankB[OTb[c]]])
                        if is_sample:
                            S.pe(lambda j=j, first=first, lastf=lastf: nc.tensor.matmul(
                                banks[SMb[c]][:, qlo * 128:ntok], ones_b[:nk, :], P_[:nk, j * ncol:(j + 1) * ncol], start=first, stop=lastf),
                                reads=[PB_, constB], writes=[bankB[SMb[c]]])
                        else:
                            PA, PAB = f32a[c], f32aB[c]
                            if first:
                                S.dve(lambda PA=PA: nc.vector.tensor_copy(out=PA[:, qlo * 128:ntok], in_=P_[:, 0:ncol]), reads=[PB_], writes=[PAB])
                            else:
                                S.dve(lambda PA=PA: nc.vector.tensor_tensor(out=PA[:, qlo * 128:ntok], in0=PA[:, qlo * 128:ntok], in1=P_[:, 0:ncol], op=ALU.add),
                                      reads=[PB_, PAB], writes=[PAB])

                DEPTH = 2
                for i in range(len(steps) + DEPTH):
                    if i < len(steps):
                        emit_st(*steps[i])
                    if i >= DEPTH:
                        emit_pv(*steps[i - DEPTH])
                A0, A1, A0B, A1B = f32a[0], f32a[1], f32aB[0], f32aB[1]
                if not is_sample:
                    for c_ in range(2):
                        ti_ = rr("tb", 3)
                        PBF, PBFB = tb[ti_][:, 0, :], tbB[ti_]
                        S.dve(lambda c_=c_, PBF=PBF: nc.vector.tensor_copy(out=PBF[:, 0:W], in_=f32a[c_][:, 0:W]), reads=[f32aB[c_]], writes=[PBFB])
                        S.pe(lambda c_=c_, PBF=PBF: nc.tensor.matmul(banks[SMb[c_]][:, 0:W], ones_b[:, :], PBF[:, 0:W], start=True, stop=True),
                             reads=[PBFB, constB], writes=[bankB[SMb[c_]]])
                S.dve(lambda: nc.vector.tensor_copy(out=A0[:, 0:W], in_=banks[SMb[0]][:, 0:W]), reads=[bankB[SMb[0]]], writes=[A0B])
                S.dve(lambda: nc.vector.tensor_copy(out=A1[:, 0:W], in_=banks[SMb[1]][:, 0:W]), reads=[bankB[SMb[1]]], writes=[A1B])
                S.dve(lambda: nc.vector.tensor_tensor(out=f32b[:, 0:W], in0=banks[OTb[0]][:, 0:W], in1=A1[:, 0:W], op=ALU.mult), reads=[bankB[OTb[0]], A1B], writes=[f32bB])
                S.dve(lambda: nc.vector.tensor_tensor(out=f32c[:, 0:W], in0=banks[OTb[1]][:, 0:W], in1=A0[:, 0:W], op=ALU.mult), reads=[bankB[OTb[1]], A0B], writes=[f32cB])
                S.dve(lambda: nc.vector.scalar_tensor_tensor(out=f32b[:, 0:W], in0=f32c[:, 0:W], scalar=nlam[:, 0:1], in1=f32b[:, 0:W], op0=ALU.mult, op1=ALU.add),
                      reads=[f32bB, f32cB, constB], writes=[f32bB])
                S.pool(lambda: nc.gpsimd.tensor_tensor(out=A0[:, 0:W], in0=A0[:, 0:W], in1=A1[:, 0:W], op=ALU.mult), reads=[A0B, A1B], writes=[A0B])
                S.pool(lambda: nc.gpsimd.tensor_tensor(out=A0[:, 0:W], in0=A0[:, 0:W], in1=A0[:, 0:W], op=ALU.mult), reads=[A0B], writes=[A0B])
                ti = rr("tb", 3)
                SQ, SQB = tb[ti][:, 0, :], tbB[ti]
                S.act(lambda: nc.scalar.activation(out=SQ[:, 0:W], in_=f32b[:, 0:W], func=AF.Square), reads=[f32bB], writes=[SQB])
                bk = rr("sbank", 4)
                S.pe(lambda: nc.tensor.matmul(banks[bk][:, 0:W], ones_b[:, :], SQ[:, 0:W], start=True, stop=True), reads=[SQB, constB], writes=[bankB[bk]])
                S.dve(lambda: nc.vector.scalar_tensor_tensor(out=f32c[:, 0:W], in0=A0[:, 0:W], scalar=EPS * 128.0, in1=banks[bk][:, 0:W], op0=ALU.mult, op1=ALU.add),
                      reads=[A0B, bankB[bk]], writes=[f32cB])
                S.act(lambda: nc.scalar.activation(out=f32c[:, 0:W], in_=f32c[:, 0:W], func=AF.Ln, scale=1.0 / 128), reads=[f32cB], writes=[f32cB])
                S.act(lambda: nc.scalar.activation(out=f32c[:, 0:W], in_=f32c[:, 0:W], func=AF.Exp, scale=-0.5), reads=[f32cB], writes=[f32cB])
                S.dve(lambda: nc.vector.scalar_tensor_tensor(out=mixT[:, 4 + h, 0:W], in0=f32b[:, 0:W], scalar=dg_sb[:, 0:1], in1=f32c[:, 0:W], op0=ALU.mult, op1=ALU.mult),
                      reads=[f32bB, f32cB, constB], writes=[mdB[h]])

            for t in range(ntiles):
                transposes([(mix[:, t, 0:512], 4, mixB[t], mixT[:, 0:4, t * 128:t * 128 + nt], [mixTB[t]])], nt)
            get_unit(ubase + 7)
            for t in range(ntiles):
                for o in range(2):
                    slot = (ubase + 7 + o) % NSLOT
                    wv = wsl[slot][:, :].rearrange("p (kk c) -> p kk c", kk=4)
                    for kk in range(4):
                        kc = 4 * o + kk
                        for hh in range(2):
                            bk = 2 * t + hh
                            S.pe(lambda kk=kk, kc=kc, t=t, hh=hh, bk=bk, wv=wv: nc.tensor.matmul(
                                banks[bk][:nt, :], mixT[:, kc, t * 128:t * 128 + nt], wv[:, kk, hh * 512:(hh + 1) * 512],
                                start=(kc == 0), stop=(kc == NKC - 1)),
                                reads=[wB[slot], mixTB[t] if kc < 4 else mdB[kc - 4]], writes=[bankB[bk]])
            postnorm_residual(ntiles, nt, 1, 1.0)

        def group(gi, xin, yout, kout, vout, tab_ap, ntiles, nt, kbase, is_sample, retout, mid_hook=None):
            ubase = gi * NU
            for t in range(ntiles):
                S.dma(S.qsync, x[:nt, t, :], xin[t * 128:t * 128 + nt, :], writes=[xB[t]])
            S.dma(S.qsync, tab[:nt, 0:ntiles, :], tab_ap.rearrange("(t p) c -> p t c", p=nt), writes=[tabB])
            ffn(ntiles, nt, ubase, 0)
            mixer(ntiles, nt, ubase + 22, kbase, is_sample, kout, vout, retout)
            inter = mid_hook() if mid_hook is not None else None
            ffn(ntiles, nt, ubase + 31, 2, inter)
            for t in range(ntiles):
                S.dma(S.qsync, yout[t * 128:t * 128 + nt, :], x[:nt, t, :], reads=[xB[t]], writes=[])

        stgB = [Buf("stg0"), Buf("stg1")]

        def sample_prefetch_early():
            S.dma(S.qsync, Sst[:, :, :], sret.rearrange("h d e -> d h e"), writes=[SstB])
            S.act(lambda: nc.scalar.copy(out=Sb[:, :, :], in_=Sst[:, :, :]), reads=[SstB], writes=[SbB])
            for kt_ in range(KT_P):
                S.dma(S.qpool, Vaug[:, kt_, :, 0:128],
                      cv[kt_ * 128:(kt_ + 1) * 128, :].rearrange("p (h d) -> p h d", h=4), writes=[VB[kt_]])
            for bi in range(min(2, knb)):
                kload(bi)
            return None

        kstg = (mix[:, 0:4, 0:512], mix[:, 0:4, 512:1024])
        knb = (KT_P + 3) // 4

        def kload(bi):
            k4 = bi * 4
            n4 = min(4, KT_P - k4)
            S.dma(S.qpool, kstg[bi % 2][:, 0:n4, :], ck[k4 * 128:(k4 + n4) * 128, :].rearrange("(k p) c -> p k c", p=128),
                  writes=[stgB[bi % 2]] + (mixB if bi < 2 else []))

        def sample_prefetch_k():
            for bi in range(knb):
                k4 = bi * 4
                n4 = min(4, KT_P - k4)
                for i in range(n4):
                    bk = rr("bank", 8)
                    src = kstg[bi % 2][:, i, :]
                    for j in range(4):
                        S.pe(lambda j=j, bk=bk, src=src: nc.tensor.transpose(bank16[bk][:, j * 128:(j + 1) * 128], src[:, j * 128:(j + 1) * 128], ident[:, :]),
                             reads=[stgB[bi % 2], constB] + mixB, writes=[bankB[bk]])
                    kk = k4 + i
                    if kk % 2 == 0:
                        S.dve(lambda bk=bk, kk=kk: nc.vector.tensor_copy(out=KtH[:, :, kk * 128:(kk + 1) * 128],
                                                                         in_=bank16[bk][:, 0:512].rearrange("p (k c) -> p k c", k=4)),
                              reads=[bankB[bk]], writes=[KtB[kk]])
                    else:
                        S.act(lambda bk=bk, kk=kk: nc.scalar.copy(out=KtH[:, :, kk * 128:(kk + 1) * 128],
                                                                  in_=bank16[bk][:, 0:512].rearrange("p (k c) -> p k c", k=4)),
                              reads=[bankB[bk]], writes=[KtB[kk]])
                if bi + 2 < knb:
                    kload(bi + 2)

        S.dve(lambda: nc.vector.memset(st[:], 0.0), writes=[stB, stQ_g] + stT + stP)
        S.dve(lambda: nc.vector.memset(Sst[:], 0.0), writes=[SstB])
        S.dve(lambda: nc.vector.memset(Sb[:], 0.0), writes=[SbB])
        for g in range(NG):
            r0 = g * 512
            group(g, xp[r0:r0 + 512, :], yp[r0:r0 + 512, :], kp[r0:r0 + 512, :], vp[r0:r0 + 512, :], tabp[r0:r0 + 512, :],
                  4, 128, 4 * g, False, retp if g == NG - 1 else None, sample_prefetch_early if g == NG - 1 else None)
        sample_prefetch_k()
        group(NG, xs_in, ys, ks, vs, tabs, 1, 64, KT_P, True, rets)
        S.finish()
    return nc


def _tables(NP, PAST):
    def rope_tab(pos, theta, rot_dim):
        half = rot_dim // 2
        inv = np.power(np.float64(theta), -np.arange(half, dtype=np.float64) * (2.0 / rot_dim))
        ang = pos.astype(np.float64)[:, None] * inv[None, :]
        return np.cos(ang), np.sin(ang)

    def tab(pos):
        cr, sr = rope_tab(pos, 10000.0, 128)
        cd, sd = rope_tab(pos, 500000.0, 16)
        s = 128.0 ** -0.5
        return np.ascontiguousarray(np.concatenate([cr, sr, cr * s, sr * s, cd, sd], axis=1).astype(np.float32))

    tabp = tab(np.arange(NP))
    tabs = tab(PAST + np.arange(64))
    log_g = np.log(1.0 - np.power(2.0, -5.0 - np.arange(4, dtype=np.float64)))
    idx = np.arange(128, dtype=np.float64)
    rel = idx[None, :] - idx[:, None]
    dm = np.where(rel >= 0, np.exp(log_g[:, None, None] * np.maximum(rel, 0.0)), 0.0)
    dmaskT = np.ascontiguousarray(dm.transpose(1, 0, 2).reshape(128, 512).astype(np.float32))
    cst = np.zeros((128, 20), np.float32)
    cst[:, 0:4] = np.exp(log_g[None, :] * (idx + 1.0)[:, None])
    cst[:, 4:8] = np.exp(log_g[None, :] * (127.0 - idx)[:, None])
    cst[:, 8:12] = np.exp(log_g[None, :] * np.maximum(63.0 - idx, 0.0)[:, None])
    cst[:, 12:16] = np.exp(log_g * 128.0)[None, :]
    cst[:, 16:20] = np.exp(log_g * 64.0)[None, :]
    ident = np.eye(128, dtype=np.float32)
    return tabp, tabs, dmaskT, cst, ident


_CACHE = {}


def run(NP, PAST, inputs, n_cores):
    f = lambda a: np.ascontiguousarray(np.asarray(a, dtype=np.float32))
    tabp, tabs, dmaskT, cst, ident = _tables(NP, PAST)
    shared = {
        "wg1": f(inputs["ffn1_w_gate"][0]), "wu1": f(inputs["ffn1_w_up"][0]), "wd1": f(inputs["ffn1_w_down"][0]),
        "wg2": f(inputs["ffn2_w_gate"][0]), "wu2": f(inputs["ffn2_w_up"][0]), "wd2": f(inputs["ffn2_w_down"][0]),
        "win": f(inputs["w_in"][0]), "wout": f(inputs["w_out"][0]),
        "gpre": f(np.stack([inputs["ffn1_pre_g"][0], inputs["mix_pre_g"][0], inputs["ffn2_pre_g"][0]]).reshape(3, NKC, 128).transpose(2, 0, 1)),
        "gpost": f(np.broadcast_to(np.stack([inputs["ffn1_post_g"][0], inputs["mix_post_g"][0], inputs["ffn2_post_g"][0]])[:, None, :], (3, 128, D))),
        "retg": f(np.broadcast_to(np.asarray(inputs["ret_norm_g"][0]).reshape(1, 512), (128, 512))),
        "dg": f(np.asarray(inputs["diff_norm_g"][0]).reshape(128, 1)),
        "lvec": f(np.broadcast_to(np.concatenate([np.asarray(inputs[k][0]) for k in ("diff_lq1", "diff_lk1", "diff_lq2", "diff_lk2")]).reshape(1, 256), (128, 256))),
        "tabp": tabp, "tabs": tabs, "dmask": dmaskT, "cst": cst, "ident": ident,
    }
    in_maps = []
    for c in range(n_cores):
        m = dict(shared)
        m["xp"] = f(inputs["x_prompt"][c]); m["xs"] = f(inputs["x_sample"][c])
        m["sret"] = f(inputs["state_ret"][0, c])
        m["ck"] = f(np.asarray(inputs["cache_diff_k"][0, c]).reshape(PAST, 512))
        m["cv"] = f(np.asarray(inputs["cache_diff_v"][0, c]).reshape(PAST, 512))
        in_maps.append(m)
    key = (NP, PAST)
    nc = build(NP, PAST)
    res = run_bass_kernel_spmd(nc, in_maps, core_ids=list(range(n_cores)))
    R = res.results
    st = lambda k: np.stack([np.asarray(r[k], dtype=np.float32) for r in R])
    y_p = st("yp"); y_s = st("ys")
    ret_p = st("retp")[None]; k_p = st("kp").reshape(1, n_cores, NP, 4, 2, 64); v_p = st("vp").reshape(1, n_cores, NP, 4, 128)
    ret_s = st("rets")[None]; k_s = st("ks").reshape(1, n_cores, 64, 4, 2, 64); v_s = st("vs").reshape(1, n_cores, 64, 4, 128)
    return (y_p, y_s, ret_p, k_p, v_p, ret_s, k_s, v_s)


def kernel(**inputs):
    return run(4096, 4096, inputs, 8)
```

```python
import math
from contextlib import ExitStack

import numpy as np
import concourse.bass as bass
import concourse.mybir as mybir
from concourse.bass_utils import run_bass_kernel_spmd

F32 = mybir.dt.float32
BF16 = mybir.dt.bfloat16
AF = mybir.ActivationFunctionType
ALU = mybir.AluOpType
AX = mybir.AxisListType

D = 1024
DFF = 2816
NKC = 8
NFC = 22
EPS = 1e-6
LAM_INIT = 0.8 - 0.6 * math.exp(-0.3 * 0)
TABW = 272
NSLOT = 3


class Buf:
    __slots__ = ("name", "w", "r")

    def __init__(self, name):
        self.name = name
        self.w = None
        self.r = {}


class Tok:
    __slots__ = ("key", "sem", "val")

    def __init__(self, key, sem, val):
        self.key, self.sem, self.val = key, sem, val


class Eng:
    def __init__(self, name, eng, sem, is_pe=False):
        self.name, self.eng, self.sem, self.is_pe = name, eng, sem, is_pe
        self.count = 0
        self.waited = {}


class DmaQ:
    def __init__(self, name, E, sems):
        self.name, self.E, self.sems = name, E, sems
        self.uses = [0] * len(sems)
        self.next = 0


class Sched:
    def __init__(self, nc, es):
        self.nc, self.es = nc, es
        sem = lambda n: es.enter_context(nc.semaphore(n))
        self.PE = Eng("pe", nc.tensor, sem("c_pe"), is_pe=True)
        self.ACT = Eng("act", nc.scalar, sem("c_act"))
        self.DVE = Eng("dve", nc.vector, sem("c_dve"))
        self.POOL = Eng("pool", nc.gpsimd, sem("c_pool"))
        self.SP = Eng("sp", nc.sync, sem("c_sp"))
        self.qsync = DmaQ("qs", self.SP, [sem(f"qs{i}") for i in range(24)])
        self.qpool = DmaQ("qp", self.POOL, [sem(f"qp{i}") for i in range(12)])

    def _wait(self, E, tok):
        if E.waited.get(tok.key, 0) >= tok.val:
            return
        E.eng.wait_ge(tok.sem, tok.val)
        E.waited[tok.key] = tok.val

    def _deps(self, E, reads, writes):
        for b in reads:
            if b.w is not None:
                self._dep1(E, b.w)
        for b in writes:
            if b.w is not None:
                self._dep1(E, b.w)
            for t in b.r.values():
                self._dep1(E, t)

    def _dep1(self, E, tok):
        if E.is_pe and tok.key == E.name:
            return
        self._wait(E, tok)

    def _commit(self, tok, reads, writes):
        for b in writes:
            b.w = tok
            b.r = {}
        for b in reads:
            b.r[tok.key] = tok

    def op(self, E, fn, reads=(), writes=()):
        self._deps(E, reads, writes)
        ins = fn()
        E.count += 1
        ins.then_inc(E.sem, 1)
        tok = Tok(E.name, E.sem, E.count)
        self._commit(tok, reads, writes)
        return tok

    def pe(self, fn, reads=(), writes=()):
        return self.op(self.PE, fn, reads, writes)

    def act(self, fn, reads=(), writes=()):
        return self.op(self.ACT, fn, reads, writes)

    def dve(self, fn, reads=(), writes=()):
        return self.op(self.DVE, fn, reads, writes)

    def pool(self, fn, reads=(), writes=()):
        return self.op(self.POOL, fn, reads, writes)

    def dma(self, Q, out, in_, reads=(), writes=()):
        slot = Q.next % len(Q.sems)
        Q.next += 1
        sem = Q.sems[slot]
        key = (Q.name, slot)
        if Q.uses[slot] > 0:
            self._wait(Q.E, Tok(key, sem, 16 * Q.uses[slot]))
        self._deps(Q.E, reads, writes)
        Q.E.eng.dma_start(out=out, in_=in_).then_inc(sem, 16)
        Q.uses[slot] += 1
        tok = Tok(key, sem, 16 * Q.uses[slot])
        self._commit(tok, reads, writes)
        return tok

    def finish(self):
        for Q in (self.qsync, self.qpool):
            for slot, n in enumerate(Q.uses):
                if n > 0:
                    self._wait(self.SP, Tok((Q.name, slot), Q.sems[slot], 16 * n))


def build(NP, PAST):
    assert NP % 512 == 0 and PAST % 128 == 0
    NG = NP // 512
    KT_P = PAST // 128
    KMAX = max(NP, PAST + 128)
    NKT = KMAX // 128
    nc = bass.Bass("TRN2", target_bir_lowering=False)

    def din(name, shape, dt=F32):
        return nc.dram_tensor(name, list(shape), dt, kind="ExternalInput").ap()

    def dout(name, shape):
        return nc.dram_tensor(name, list(shape), F32, kind="ExternalOutput").ap()

    def dint(name, shape, dt=BF16):
        return nc.dram_tensor(name, list(shape), dt, kind="Internal").ap()

    xp = din("xp", [NP, D]); xs_in = din("xs", [64, D])
    sret = din("sret", [4, 128, 128]); ck = din("ck", [PAST, 512]); cv = din("cv", [PAST, 512])
    wg = [din("wg1", [D, DFF]), din("wg2", [D, DFF])]
    wu = [din("wu1", [D, DFF]), din("wu2", [D, DFF])]
    wd = [din("wd1", [DFF, D]), din("wd2", [DFF, D])]
    win = din("win", [D, 3584]); wout = din("wout", [D, D])
    gpre = din("gpre", [128, 3, NKC]); gpost = din("gpost", [3, 128, D])
    retg = din("retg", [128, 512]); dgin = din("dg", [128, 1]); lvec = din("lvec", [128, 256])
    tabp = din("tabp", [NP, TABW]); tabs = din("tabs", [64, TABW])
    dmask_in = din("dmask", [128, 512]); cst_in = din("cst", [128, 20]); ident_in = din("ident", [128, 128])

    yp = dout("yp", [NP, D]); ys = dout("ys", [64, D])
    retp = dout("retp", [4, 128, 128]); kp = dout("kp", [NP, 512]); vp = dout("vp", [NP, 512])
    rets = dout("rets", [4, 128, 128]); ks = dout("ks", [64, 512]); vs = dout("vs", [64, 512])

    sc_gu = [dint(f"sc_gu{f}", [11, 128, 4096]) for f in range(2)]
    sc_d = [dint(f"sc_d{f}", [11, 128, 2048]) for f in range(2)]
    sc_in = dint("sc_in", [7, 128, 4096]); sc_out = dint("sc_out", [2, 128, 4096])

    es = ExitStack()
    with es:
        S = Sched(nc, es)
        PE, ACT, DVE, POOL = S.PE, S.ACT, S.DVE, S.POOL

        def sb(name, shape, dt=F32):
            return es.enter_context(nc.sbuf_tensor("sb_" + name, list(shape), dt))

        x = sb("x", [128, 4, D]); xB = [Buf(f"x{t}") for t in range(4)]
        xs_b = [sb(f"xs_b{i}", [128, D], BF16) for i in range(2)]; xsB = [Buf(f"xs_b{i}") for i in range(2)]
        xnT = sb("xnT", [128, NKC, 512], BF16); xnB = [Buf(f"xnT{t}") for t in range(4)]
        hT = sb("hT", [128, NFC, 512], BF16); hB = [Buf(f"hT{c}") for c in range(NFC)]
        wsl = [sb(f"wsl{i}", [128, 4096], BF16) for i in range(NSLOT)]; wB = [Buf(f"wsl{i}") for i in range(NSLOT)]
        KtH = sb("KtH", [128, 4, KMAX], BF16); KtB = [Buf(f"Kt{k}") for k in range(NKT)]
        Vaug = sb("Vaug", [128, NKT, 4, 130], BF16); VB = [Buf(f"V{k}") for k in range(NKT)]
        mix = sb("mix", [128, 4, D], BF16); mixB = [Buf(f"mix{t}") for t in range(4)]
        sgr = sb("sgr", [128, 4, 512], BF16); sgrB = [Buf(f"sgr{t}") for t in range(4)]
        kdec = sb("kdec", [128, 4, 512], BF16); kdecB = [Buf(f"kdec{t}") for t in range(4)]
        gpb = sb("gpb", [128, D]); gpbB = Buf("gpb")
        retg_sb = sb("retg_sb", [128, 512]); dg_sb = sb("dg_sb", [128, 1]); ones_b = sb("ones_b", [128, 128], BF16); dmask = sb("dmask_sb", [128, 512])
        cst = sb("cst_sb", [128, 20]); ident = sb("ident", [128, 128], BF16)
        gcol = sb("gcol", [128, 3, NKC]); lam = sb("lam", [128, 8])
        constB = Buf("consts")
        tab = sb("tab", [128, 4, TABW]); tabB = Buf("tab")
        f32a = [sb(f"f32a{i}", [128, 512]) for i in range(2)]; f32aB = [Buf(f"f32a{i}") for i in range(2)]
        f32b = sb("f32b", [128, 512]); f32bB = Buf("f32b")
        f32c = sb("f32c", [128, 512]); f32cB = Buf("f32c")
        kdf = [sb(f"kdf{i}", [128, 512]) for i in range(2)]; kdfB = [Buf(f"kdf{i}") for i in range(2)]
        vdf = [sb(f"vdf{i}", [128, 512]) for i in range(1)]; vdfB = [Buf(f"vdf{i}") for i in range(1)]
        rt = sb("rt", [128, 4, 64]); rtB = Buf("rt")
        tb = [sb(f"tb{i}", [128, 2, 512], BF16) for i in range(3)]; tbB = [Buf(f"tb{i}") for i in range(3)]
        pT = [sb(f"pT{i}", [128, 512], BF16) for i in range(3)]; pTB = [Buf(f"pT{i}") for i in range(3)]
        sTm = [sb(f"sTm{i}", [128, 128], BF16) for i in range(4)]; sTmB = [Buf(f"sTm{i}") for i in range(4)]
        sg = [sb(f"sg{i}", [128, 512], BF16) for i in range(2)]; sgB = [Buf(f"sg{i}") for i in range(2)]
        ptmp = sb("ptmp", [128, D]); ptmpB = [Buf("ptmp0"), Buf("ptmp1")]
        lv = ptmp[:, 0:256]; identf = ptmp[:, 256:384]
        pTd = [sb(f"pTd{i}", [128, 512], BF16) for i in range(3)]; pTdB = [Buf(f"pTd{i}") for i in range(3)]
        Sst = sb("Sst", [128, 4, 128]); SstB = Buf("Sst")
        Sb = sb("Sb", [128, 4, 128], BF16); SbB = Buf("Sb")
        st = sb("st", [128, 64]); stB = Buf("st")
        mdB = [Buf(f"md{h}") for h in range(4)]
        qrT = hT[:, 0:4, :]; qdecT = hT[:, 4:8, :]; krT = hT[:, 8:12, :]; QdT = hT[:, 12:16, :]; vr = hT[:, 16:20, :]
        qrTB, qdecTB, krTB, QdTB, vrB = hB[0:4], hB[4:8], hB[8:12], hB[12:16], hB[16:20]
        mixT, mixTB = xnT, xnB

        psum_all = es.enter_context(nc.psum_tensor("psum_all", [128, 4096], F32))
        banks = [psum_all[:, i * 512:(i + 1) * 512] for i in range(8)]
        bankB = [Buf(f"bank{i}") for i in range(8)]
        psum16 = psum_all[:].bitcast(BF16)
        bank16 = [psum16[:, i * 1024:(i + 1) * 1024] for i in range(8)]

        outB = Buf("outputs")

        q = S.qsync
        ctoks = []
        for dst, src in ((retg_sb, retg), (dg_sb, dgin), (dmask, dmask_in), (cst, cst_in), (identf, ident_in), (lv, lvec)):
            ctoks.append(S.dma(q, dst[:] if dst not in (identf, lv) else dst, src))
        ctoks.append(S.dma(q, gcol[:], gpre))
        for E_ in (PE, ACT, DVE, POOL):
            for tk in ctoks:
                S._wait(E_, tk)
        S.dve(lambda: nc.vector.tensor_copy(out=ident[:], in_=identf), reads=[constB], writes=[constB, ptmpB[0]])
        for i_ in range(3):
            S.dve(lambda i_=i_: nc.vector.memset(pTd[i_][:], 0.0), writes=[pTdB[i_]])
        S.dve(lambda: nc.vector.memset(ones_b[:], 1.0), reads=[constB], writes=[constB])
        S.dve(lambda: nc.vector.tensor_scalar(out=dg_sb[:], in0=dg_sb[:], scalar1=1.0 - LAM_INIT, scalar2=None, op0=ALU.mult),
              reads=[constB], writes=[constB])
        S.dve(lambda: nc.vector.tensor_tensor(out=lv[:, 0:64], in0=lv[:, 0:64], in1=lv[:, 64:128], op=ALU.mult), reads=[constB], writes=[constB, ptmpB[0]])
        S.dve(lambda: nc.vector.tensor_tensor(out=lv[:, 128:192], in0=lv[:, 128:192], in1=lv[:, 192:256], op=ALU.mult), reads=[constB], writes=[constB, ptmpB[0]])
        S.dve(lambda: nc.vector.tensor_reduce(out=lam[:, 0:1], in_=lv[:, 0:64], axis=AX.X, op=ALU.add), reads=[constB], writes=[constB, ptmpB[0]])
        S.dve(lambda: nc.vector.tensor_reduce(out=lam[:, 1:2], in_=lv[:, 128:192], axis=AX.X, op=ALU.add), reads=[constB], writes=[constB, ptmpB[0]])
        S.act(lambda: nc.scalar.activation(out=lam[:, 4:6], in_=lam[:, 0:2], func=AF.Exp), reads=[constB], writes=[constB])
        S.dve(lambda: nc.vector.tensor_tensor(out=lam[:, 2:3], in0=lam[:, 5:6], in1=lam[:, 4:5], op=ALU.subtract), reads=[constB], writes=[constB])
        S.dve(lambda: nc.vector.tensor_scalar(out=lam[:, 3:4], in0=lam[:, 2:3], scalar1=-LAM_INIT, scalar2=None, op0=ALU.add), reads=[constB], writes=[constB])
        nlam = lam[:, 3:4]
        epsc = lam[:, 6:8]
        S.dve(lambda: nc.vector.memset(lam[:, 6:7], EPS), reads=[constB], writes=[constB])
        S.dve(lambda: nc.vector.memset(lam[:, 7:8], 4.0 * EPS), reads=[constB], writes=[constB])

        def gu_unit(f, u):
            srcs = [(lambda sl, j=j: sl[:, j * 2048:(j + 1) * 2048].rearrange("p (kc c) -> p kc c", kc=8),
                     w_[f][:, u * 256:(u + 1) * 256].rearrange("(kc p) c -> p kc c", p=128)) for j, w_ in enumerate((wg, wu))]
            return (sc_gu[f][u], 4096, srcs)

        def d_unit(f, v):
            srcs = [(lambda sl: sl[:, 0:2048].rearrange("p (kk c) -> p kk c", kk=2),
                     wd[f][v * 256:(v + 1) * 256, :].rearrange("(kk p) c -> p kk c", p=128))]
            return (sc_d[f][v], 2048, srcs)

        def in_unit(b):
            srcs = [(lambda sl: sl[:, 0:4096].rearrange("p (kc c) -> p kc c", kc=8),
                     win[:, b * 512:(b + 1) * 512].rearrange("(kc p) c -> p kc c", p=128))]
            return (sc_in[b], 4096, srcs)

        def out_unit(o):
            srcs = [(lambda sl: sl[:, 0:4096].rearrange("p (kk c) -> p kk c", kk=4),
                     wout[o * 512:(o + 1) * 512, :].rearrange("(kk p) c -> p kk c", p=128))]
            return (sc_out[o], 4096, srcs)

        group_units = ([gu_unit(0, u) for u in range(11)] + [d_unit(0, v) for v in range(11)]
                       + [in_unit(b) for b in (0, 2, 1, 3, 5, 6, 4)] + [out_unit(o) for o in range(2)]
                       + [gu_unit(1, u) for u in range(11)] + [d_unit(1, v) for v in range(11)])
        NU = len(group_units)
        scB = [Buf(f"sc{i}") for i in range(NU)]
        n_groups_total = NG + 1
        wstate = {"loaded": 0}

        def emit_load(gi):
            g, ui = divmod(gi, NU)
            sc_ap, n, srcs = group_units[ui]
            slot = gi % NSLOT
            if g == 0:
                for viewfn, src in srcs:
                    S.dma(S.qpool, viewfn(wsl[slot]), src, writes=[wB[slot]])
                S.dma(S.qsync, sc_ap, wsl[slot][:, 0:n], reads=[wB[slot]], writes=[scB[ui]])
            else:
                S.dma(S.qsync, wsl[slot][:, 0:n], sc_ap, reads=[scB[ui]], writes=[wB[slot]])

        def get_unit(gi):
            lim = min(gi + NSLOT, n_groups_total * NU)
            while wstate["loaded"] < lim:
                emit_load(wstate["loaded"])
                wstate["loaded"] += 1
            return gi % NSLOT

        cnt = {"bank": 0, "sbank": 0, "tb": 0, "pT": 0, "pTd": 0, "sTm": 0, "sg": 0, "f32a": 0, "kdf": 0, "vdf": 0}

        def rr(name, n):
            v = cnt[name] % n
            cnt[name] += 1
            return v

        def rstd_from_sumsq(ss_ap, out_ap, n_feat, rows, cols_buf):
            S.dve(lambda: nc.vector.tensor_scalar(out=out_ap, in0=ss_ap, scalar1=1.0 / n_feat, scalar2=EPS, op0=ALU.mult, op1=ALU.add),
                  reads=[stB], writes=[stB])
            S.act(lambda: nc.scalar.activation(out=out_ap, in_=out_ap, func=AF.Sqrt), reads=[stB], writes=[stB])
            S.dve(lambda: nc.vector.reciprocal(out=out_ap, in_=out_ap), reads=[stB], writes=[stB])

        stQ_g = Buf("stQ")
        stT = [Buf(f"stT{t}") for t in range(4)]
        stP = [Buf(f"stP{t}") for t in range(4)]

        def prenorm_T(ntiles, nt, gi_):
            def stage1(t):
                xi = t % 2
                S.act(lambda: nc.scalar.activation(out=xs_b[xi][:nt, :], in_=x[:nt, t, :], func=AF.Square, accum_out=st[:nt, t:t + 1]),
                      reads=[xB[t], stT[t]], writes=[xsB[xi], stT[t]])
                S.act(lambda: nc.scalar.activation(out=st[:nt, 4 + t:5 + t], in_=st[:nt, t:t + 1], func=AF.Sqrt, scale=1.0 / D, bias=epsc[:nt, 0:1]),
                      reads=[stT[t], constB], writes=[stT[t]])
                S.dve(lambda: nc.vector.reciprocal(out=st[:nt, 4 + t:5 + t], in_=st[:nt, 4 + t:5 + t]), reads=[stT[t]], writes=[stT[t]])
                S.dve(lambda: nc.vector.tensor_scalar(out=xs_b[xi][:nt, :], in0=x[:nt, t, :], scalar1=st[:nt, 4 + t:5 + t], scalar2=None, op0=ALU.mult),
                      reads=[xB[t], stT[t]], writes=[xsB[xi]])

            def stage2(t):
                xi = t % 2
                bk = rr("bank", 8)
                for kc in range(NKC):
                    S.pe(lambda kc=kc: nc.tensor.transpose(bank16[bk][:, kc * 128:kc * 128 + nt], xs_b[xi][:nt, kc * 128:(kc + 1) * 128], ident[:nt, :nt]),
                         reads=[xsB[xi], constB], writes=[bankB[bk]])
                S.dve(lambda: nc.vector.tensor_tensor(
                    out=xnT[:, :, t * 128:t * 128 + nt],
                    in0=bank16[bk].rearrange("p (kc c) -> p kc c", kc=8)[:, :, 0:nt],
                    in1=gcol[:, gi_, :].unsqueeze(2).to_broadcast([128, NKC, nt]), op=ALU.mult),
                    reads=[bankB[bk], constB], writes=[xnB[t]] + mdB)

            def warm(t):
                xi = t % 2
                wbk = rr("bank", 8)
                for _ in range(4):
                    S.pe(lambda: nc.tensor.matmul(banks[wbk][:, 0:512], ident[:nt, :], xs_b[xi][:nt, 0:512], start=True, stop=True),
                         reads=[xsB[xi], constB], writes=[bankB[wbk]])

            for t in range(ntiles + 1):
                if t < ntiles:
                    stage1(t)
                if t >= 1:
                    stage2(t - 1)
                    if ntiles > 1:
                        warm(t - 1)

        def postnorm_residual(ntiles, nt, gidx, half):
            S.dma(S.qsync, gpb[:], gpost[gidx], writes=[gpbB])
            a = 1.0 / (half * half)
            ecol = 0 if half == 1.0 else 1
            for t in range(ntiles):
                xi = t % 2
                yv = psum_all[:nt, 2 * t * 512:(2 * t + 2) * 512]
                yB = [bankB[2 * t], bankB[2 * t + 1]]
                S.act(lambda t=t, xi=xi, yv=yv: nc.scalar.activation(out=xs_b[xi][:nt, :], in_=yv, func=AF.Square, accum_out=st[:nt, 8 + t:9 + t]),
                      reads=yB + [stP[t]], writes=[xsB[xi], stP[t]])
                S.act(lambda t=t: nc.scalar.activation(out=st[:nt, 16 + t:17 + t], in_=st[:nt, 8 + t:9 + t], func=AF.Sqrt, scale=a / D, bias=epsc[:nt, ecol:ecol + 1]),
                      reads=[stP[t], constB], writes=[stP[t]])
                S.dve(lambda t=t: nc.vector.reciprocal(out=st[:nt, 20 + t:21 + t], in_=st[:nt, 16 + t:17 + t]), reads=[stP[t]], writes=[stP[t]])
                if t >= 1 and ntiles > 1:
                    for _ in range(3):
                        S.pe(lambda xi=xi: nc.tensor.matmul(banks[0][:, 0:512], ident[:nt, :], xs_b[xi][:nt, 0:512], start=True, stop=True),
                             reads=[xsB[xi], constB], writes=[bankB[0]])
                for hh in range(2):
                    bk = 2 * t + hh
                    S.dve(lambda t=t, hh=hh, bk=bk: nc.vector.scalar_tensor_tensor(
                        out=ptmp[:nt, hh * 512:(hh + 1) * 512], in0=banks[bk][:nt, :], scalar=st[:nt, 20 + t:21 + t],
                        in1=gpb[:nt, hh * 512:(hh + 1) * 512], op0=ALU.mult, op1=ALU.mult),
                        reads=[bankB[bk], stP[t], gpbB], writes=[ptmpB[hh]])
                    if (2 * t + hh) % 3 == 2:
                        S.dve(lambda t=t, hh=hh: nc.vector.tensor_tensor(out=x[:nt, t, hh * 512:(hh + 1) * 512], in0=ptmp[:nt, hh * 512:(hh + 1) * 512],
                                                                         in1=x[:nt, t, hh * 512:(hh + 1) * 512], op=ALU.add),
                              reads=[ptmpB[hh], xB[t]], writes=[xB[t]])
                    else:
                        S.pool(lambda t=t, hh=hh: nc.gpsimd.tensor_tensor(out=x[:nt, t, hh * 512:(hh + 1) * 512], in0=ptmp[:nt, hh * 512:(hh + 1) * 512],
                                                                          in1=x[:nt, t, hh * 512:(hh + 1) * 512], op=ALU.add),
                               reads=[ptmpB[hh], xB[t]], writes=[xB[t]])

        def ffn(ntiles, nt, ubase, gidx, inter=None):
            ntok = (ntiles - 1) * 128 + nt
            prenorm_T(ntiles, nt, gidx)
            for u in range(11):
                slot = get_unit(ubase + u)
                wv = wsl[slot][:, :].rearrange("p (j kc c) -> p j kc c", j=2, kc=8)
                for cc in range(2):
                    c = 2 * u + cc
                    bg, bu = rr("bank", 8), rr("bank", 8)
                    for j, bk in ((0, bg), (1, bu)):
                        for kc in range(NKC):
                            S.pe(lambda j=j, bk=bk, kc=kc, cc=cc, wv=wv: nc.tensor.matmul(
                                banks[bk][:, 0:ntok], wv[:, j, kc, cc * 128:(cc + 1) * 128], xnT[:, kc, 0:ntok],
                                start=(kc == 0), stop=(kc == NKC - 1)),
                                reads=[wB[slot]] + xnB[0:ntiles] + mdB, writes=[bankB[bk]])
                    si = rr("sg", 2)
                    S.act(lambda bg=bg, si=si: nc.scalar.activation(out=sg[si][:, 0:ntok], in_=banks[bg][:, 0:ntok], func=AF.Silu),
                          reads=[bankB[bg]], writes=[sgB[si]])
                    S.dve(lambda bu=bu, si=si, c=c: nc.vector.tensor_tensor(out=hT[:, c, 0:ntok], in0=banks[bu][:, 0:ntok], in1=sg[si][:, 0:ntok], op=ALU.mult),
                          reads=[bankB[bu], sgB[si]], writes=[hB[c]])
                if inter:
                    inter.pop(0)()
            while inter:
                inter.pop(0)()
            def down_mm(slot, kk, kc, t, hh):
                wv = wsl[slot][:, 0:2048].rearrange("p (kk c) -> p kk c", kk=2)
                bk = 2 * t + hh
                S.pe(lambda: nc.tensor.matmul(banks[bk][:nt, :], hT[:, kc, t * 128:t * 128 + nt], wv[:, kk, hh * 512:(hh + 1) * 512],
                                              start=(kc == 0), stop=(kc == NFC - 1)),
                     reads=[wB[slot], hB[kc]], writes=[bankB[bk]])

            NTAIL = 3 if ntiles > 1 else 0
            for v in range(11 - NTAIL):
                slot = get_unit(ubase + 11 + v)
                for kk in range(2):
                    for t in range(ntiles):
                        for hh in range(2):
                            down_mm(slot, kk, 2 * v + kk, t, hh)
            if NTAIL:
                g0 = ubase + 11 + (11 - NTAIL)
                get_unit(g0)
                for t in range(ntiles):
                    for v in range(11 - NTAIL, 11):
                        slot = (ubase + 11 + v) % NSLOT
                        for kk in range(2):
                            for hh in range(2):
                                down_mm(slot, kk, 2 * v + kk, t, hh)
            postnorm_residual(ntiles, nt, gidx, 0.5)

        def transposes(srcs, nt, evac="dve"):
            for src, k, sB, dst, dB in srcs:
                bk = rr("bank", 8)
                for i in range(k):
                    S.pe(lambda i=i, bk=bk, src=src: nc.tensor.transpose(bank16[bk][:, i * 128:i * 128 + nt], src[:nt, i * 128:(i + 1) * 128], ident[:nt, :nt]),
                         reads=[sB, constB], writes=[bankB[bk]])
                if evac == "act":
                    S.act(lambda bk=bk, k=k, dst=dst: nc.scalar.copy(out=dst, in_=bank16[bk][:, 0:k * 128].rearrange("p (k c) -> p k c", k=k)[:, :, 0:nt]),
                          reads=[bankB[bk]], writes=dB)
                else:
                    S.dve(lambda bk=bk, k=k, dst=dst: nc.vector.tensor_copy(out=dst, in_=bank16[bk][:, 0:k * 128].rearrange("p (k c) -> p k c", k=k)[:, :, 0:nt]),
                          reads=[bankB[bk]], writes=dB)

        def rope_full(src, srcB, nt, t, coff, outs):
            V8 = lambda a: a[:nt, :].rearrange("p (m d) -> p m d", d=64)
            V42 = lambda a: a[:nt, :].rearrange("p (h two d) -> p h two d", h=4, two=2)
            cosb8 = tab[:nt, t, coff:coff + 64].unsqueeze(1).to_broadcast([nt, 8, 64])
            sinb4 = tab[:nt, t, coff + 64:coff + 128].unsqueeze(1).to_broadcast([nt, 4, 64])
            S.dve(lambda: nc.vector.tensor_tensor(out=V8(f32b), in0=V8(src), in1=cosb8, op=ALU.mult), reads=[srcB, tabB], writes=[f32bB])
            S.pool(lambda: nc.gpsimd.tensor_tensor(out=V42(f32c)[:, :, 0, :], in0=V42(src)[:, :, 1, :], in1=sinb4, op=ALU.mult), reads=[srcB, tabB], writes=[f32cB])
            S.pool(lambda: nc.gpsimd.tensor_tensor(out=V42(f32c)[:, :, 1, :], in0=V42(src)[:, :, 0, :], in1=sinb4, op=ALU.mult), reads=[srcB, tabB], writes=[f32cB])
            S.dve(lambda: nc.vector.tensor_tensor(out=V42(src)[:, :, 0, :], in0=V42(f32b)[:, :, 0, :], in1=V42(f32c)[:, :, 0, :], op=ALU.subtract),
                  reads=[f32bB, f32cB], writes=[srcB])
            S.dve(lambda: nc.vector.tensor_tensor(out=V42(src)[:, :, 1, :], in0=V42(f32b)[:, :, 1, :], in1=V42(f32c)[:, :, 1, :], op=ALU.add),
                  reads=[f32bB, f32cB], writes=[srcB])

        def rope_partial(src, srcB, nt, t):
            V = src[:nt, :].rearrange("p (m d) -> p m d", d=64)
            X1, X2 = V[:, :, 0:8], V[:, :, 8:16]
            cosb = tab[:nt, t, 256:264].unsqueeze(1).to_broadcast([nt, 8, 8])
            sinb = tab[:nt, t, 264:272].unsqueeze(1).to_broadcast([nt, 8, 8])
            R = [rt[:nt, i, :].rearrange("p (m d) -> p m d", d=8) for i in range(4)]
            S.dve(lambda: nc.vector.tensor_tensor(out=R[0], in0=X1, in1=cosb, op=ALU.mult), reads=[srcB, tabB], writes=[rtB])
            S.dve(lambda: nc.vector.tensor_tensor(out=R[1], in0=X2, in1=sinb, op=ALU.mult), reads=[srcB, tabB], writes=[rtB])
            S.dve(lambda: nc.vector.tensor_tensor(out=R[2], in0=X2, in1=cosb, op=ALU.mult), reads=[srcB, tabB], writes=[rtB])
            S.dve(lambda: nc.vector.tensor_tensor(out=R[3], in0=X1, in1=sinb, op=ALU.mult), reads=[srcB, tabB], writes=[rtB])
            S.dve(lambda: nc.vector.tensor_tensor(out=X1, in0=R[0], in1=R[1], op=ALU.subtract), reads=[rtB], writes=[srcB])
            S.dve(lambda: nc.vector.tensor_tensor(out=X2, in0=R[2], in1=R[3], op=ALU.add), reads=[rtB], writes=[srcB])

        def mixer(ntiles, nt, ubase, kbase, is_sample, kout, vout, retout):
            ntok = (ntiles - 1) * 128 + nt
            prenorm_T(ntiles, nt, 1)
            qdec_c = cst[:, 0:4]
            kdec_c = cst[:, 8:12] if is_sample else cst[:, 4:8]
            blk_c = cst[:, 16:20] if is_sample else cst[:, 12:16]
            pending = []
            stepc = {"i": 0}

            def flush(all_=False):
                while pending and (all_ or pending[0][0] <= stepc["i"] - 3):
                    pending.pop(0)[1]()

            Qp = (hT[:, 12:16, :], xnT[:, 0:4, :])

            def q_transposes(T_, TB_, t):
                tsl_ = slice(t * 128, t * 128 + nt)
                bk = rr("bank", 8)
                for i in range(4):
                    S.pe(lambda i=i: nc.tensor.transpose(bank16[bk][:, i * 128:i * 128 + nt], T_[:nt, 0, i * 128:(i + 1) * 128], ident[:nt, :nt]),
                         reads=[TB_, constB], writes=[bankB[bk]])
                src = bank16[bk][:, 0:512].rearrange("p (k c) -> p k c", k=4)
                S.pool(lambda: nc.gpsimd.memset(Qp[0][64:128, :, tsl_], 0.0), writes=QdTB)
                S.pool(lambda: nc.gpsimd.memset(Qp[1][0:64, :, tsl_], 0.0), writes=[xnB[t]])
                S.act(lambda: nc.scalar.copy(out=Qp[0][0:64, :, tsl_], in_=src[0:64, :, 0:nt]), reads=[bankB[bk]], writes=QdTB)
                S.act(lambda: nc.scalar.copy(out=Qp[1][64:128, :, tsl_], in_=src[64:128, :, 0:nt]), reads=[bankB[bk]], writes=[xnB[t]])

            ret_state = {}
            stQ = stQ_g

            def ret_P1(t):
                bs, bkv = rr("bank", 8), rr("bank", 8)
                tsl = slice(t * 128, t * 128 + nt)
                for h in range(4):
                    S.pe(lambda h=h: nc.tensor.matmul(banks[bs][:nt, h * 128:h * 128 + nt], krT[:, h, tsl], qrT[:, h, tsl], start=True, stop=True),
                         reads=[krTB[h], qrTB[h]], writes=[bankB[bs]])
                for h in range(4):
                    S.pe(lambda h=h: nc.tensor.matmul(banks[bkv][:, h * 128:(h + 1) * 128], kdec[:nt, t, h * 128:(h + 1) * 128], vr[:nt, t, h * 128:(h + 1) * 128],
                                                      start=True, stop=True),
                         reads=[kdecB[t], vrB[t]], writes=[bankB[bkv]])
                for h in range(4):
                    S.dve(lambda h=h: nc.vector.tensor_tensor(out=sTm[h][:nt, :nt], in0=banks[bs][:nt, h * 128:h * 128 + nt], in1=dmask[:nt, h * 128:h * 128 + nt], op=ALU.mult),
                          reads=[bankB[bs], constB], writes=[sTmB[h]])
                ret_state[t] = bkv

            def ret_P2(t):
                bkv = ret_state.pop(t)
                bo = rr("bank", 8)
                tsl = slice(t * 128, t * 128 + nt)
                for h in range(4):
                    S.pe(lambda h=h: nc.tensor.matmul(banks[bo][:nt, h * 128:(h + 1) * 128], sTm[h][:nt, :nt], vr[:nt, t, h * 128:(h + 1) * 128], start=True, stop=False),
                         reads=[sTmB[h], vrB[t]], writes=[bankB[bo]])
                    S.pe(lambda h=h: nc.tensor.matmul(banks[bo][:nt, h * 128:(h + 1) * 128], qdecT[:, h, tsl], Sb[:, h, :], start=False, stop=True),
                         reads=[qdecTB[h], SbB], writes=[bankB[bo]])
                for h in range(4):
                    S.dve(lambda h=h: nc.vector.scalar_tensor_tensor(out=Sst[:, h, :], in0=Sst[:, h, :], scalar=blk_c[:, h:h + 1], in1=banks[bkv][:, h * 128:(h + 1) * 128],
                                                                     op0=ALU.mult, op1=ALU.add),
                          reads=[SstB, bankB[bkv], constB], writes=[SstB])
                S.act(lambda: nc.scalar.copy(out=Sb[:, :, :], in_=Sst[:, :, :]), reads=[SstB], writes=[SbB])
                RO, ROB = f32b, f32bB
                S.act(lambda: nc.scalar.copy(out=RO[:nt, :], in_=banks[bo][:nt, :]), reads=[bankB[bo]], writes=[ROB])

            def ret_B(t):
                RO, ROB = f32b, f32bB
                V4 = lambda a_: a_[:nt, :].rearrange("p (h d) -> p h d", h=4)
                S.dve(lambda: nc.vector.tensor_reduce(out=st[:nt, 24:28], in_=V4(RO), axis=AX.X, op=ALU.add), reads=[ROB], writes=[stB])
                for h in range(4):
                    S.act(lambda h=h: nc.scalar.activation(out=f32c[:nt, h * 128:(h + 1) * 128], in_=RO[:nt, h * 128:(h + 1) * 128], func=AF.Square,
                                                           accum_out=st[:nt, 48 + h:49 + h]),
                          reads=[ROB, stQ], writes=[f32cB, stQ])
                S.dve(lambda: nc.vector.tensor_copy(out=st[:nt, 28:32], in_=st[:nt, 48:52]), reads=[stQ, stB], writes=[stB])
                S.dve(lambda: nc.vector.tensor_scalar(out=st[:nt, 24:28], in0=st[:nt, 24:28], scalar1=1.0 / 128, scalar2=None, op0=ALU.mult), reads=[stB], writes=[stB])
                S.dve(lambda: nc.vector.tensor_tensor(out=st[:nt, 32:36], in0=st[:nt, 24:28], in1=st[:nt, 24:28], op=ALU.mult), reads=[stB], writes=[stB])
                S.dve(lambda: nc.vector.scalar_tensor_tensor(out=st[:nt, 28:32], in0=st[:nt, 28:32], scalar=1.0 / 128, in1=st[:nt, 32:36], op0=ALU.mult, op1=ALU.subtract),
                      reads=[stB], writes=[stB])
                S.act(lambda: nc.scalar.activation(out=st[:nt, 28:32], in_=st[:nt, 28:32], func=AF.Sqrt, scale=1.0, bias=epsc[:nt, 0:1]), reads=[stB, constB], writes=[stB])
                S.dve(lambda: nc.vector.reciprocal(out=st[:nt, 28:32], in_=st[:nt, 28:32]), reads=[stB], writes=[stB])
                S.dve(lambda: nc.vector.scalar_tensor_tensor(out=st[:nt, 32:36], in0=st[:nt, 24:28], scalar=-1.0, in1=st[:nt, 28:32], op0=ALU.mult, op1=ALU.mult),
                      reads=[stB], writes=[stB])
                for h in range(4):
                    S.dve(lambda h=h: nc.vector.tensor_scalar(out=f32c[:nt, h * 128:(h + 1) * 128], in0=RO[:nt, h * 128:(h + 1) * 128],
                                                              scalar1=st[:nt, 28 + h:29 + h], scalar2=st[:nt, 32 + h:33 + h], op0=ALU.mult, op1=ALU.add),
                          reads=[ROB, stB], writes=[f32cB])
                S.pool(lambda: nc.gpsimd.tensor_tensor(out=f32c[:nt, :], in0=f32c[:nt, :], in1=retg_sb[:nt, :], op=ALU.mult), reads=[f32cB, constB], writes=[f32cB])
                S.pool(lambda: nc.gpsimd.tensor_tensor(out=mix[:nt, t, 0:512], in0=f32c[:nt, :], in1=sgr[:nt, t, :], op=ALU.mult), reads=[f32cB, sgrB[t]], writes=[mixB[t]])

            ret_queue = []
            for t_ in range(ntiles):
                if t_ == 0:
                    ret_queue.append(lambda: ret_P1(0))
                else:
                    ret_queue.append(lambda t_=t_: (ret_P1(t_), ret_B(t_ - 1)))
                ret_queue.append(lambda t_=t_: ret_P2(t_))
            ret_queue.append(lambda: ret_B(ntiles - 1))

            for bpos in range(7):
                slot = get_unit(ubase + bpos)
                wv = wsl[slot][:, :].rearrange("p (kc c) -> p kc c", kc=8)
                b = (0, 2, 1, 3, 5, 6, 4)[bpos]
                for t in range(ntiles):
                    bk = rr("bank", 8)
                    for kc in range(NKC):
                        S.pe(lambda kc=kc, t=t, bk=bk, wv=wv: nc.tensor.matmul(
                            banks[bk][:nt, :], xnT[:, kc, t * 128:t * 128 + nt], wv[:, kc, :], start=(kc == 0), stop=(kc == NKC - 1)),
                            reads=[wB[slot], xnB[t]] + mdB, writes=[bankB[bk]])
                    stepc["i"] += 1
                    flush()
                    if bpos >= 4 and ret_queue:
                        if bpos == 4 and t == 0:
                            flush(True)
                        ret_queue.pop(0)()
                    kt = kbase + t
                    if b in (0, 1):
                        ai = rr("f32a", 2); A, AB = f32a[ai], f32aB[ai]
                        S.act(lambda bk=bk, A=A: nc.scalar.copy(out=A[:nt, :], in_=banks[bk][:nt, :]), reads=[bankB[bk]], writes=[AB])
                        rope_full(A, AB, nt, t, 0 if b == 0 else 128, None)
                        ti = rr("tb", 3); T_, TB_ = tb[ti], tbB[ti]
                        S.act(lambda A=A, T_=T_: nc.scalar.copy(out=T_[:nt, 0, :], in_=A[:nt, :]), reads=[AB], writes=[TB_])
                        if b == 0:
                            S.pool(lambda A=A, T_=T_: nc.gpsimd.tensor_tensor(
                                out=T_[:nt, 1, :].rearrange("p (h d) -> p h d", h=4), in0=A[:nt, :].rearrange("p (h d) -> p h d", h=4),
                                in1=qdec_c[:nt, :].unsqueeze(2).to_broadcast([nt, 4, 128]), op=ALU.mult), reads=[AB, constB], writes=[TB_])
                            pending.append((stepc["i"], lambda T_=T_, TB_=TB_, t=t: transposes(
                                [(T_[:, 0, :], 4, TB_, qrT[:, :, t * 128:t * 128 + nt], [qrTB[h] for h in range(4)]),
                                 (T_[:, 1, :], 4, TB_, qdecT[:, :, t * 128:t * 128 + nt], [qdecTB[h] for h in range(4)])], nt, "act")))
                        else:
                            S.pool(lambda A=A, t=t: nc.gpsimd.tensor_tensor(
                                out=kdec[:nt, t, :].rearrange("p (h d) -> p h d", h=4), in0=A[:nt, :].rearrange("p (h d) -> p h d", h=4),
                                in1=kdec_c[:nt, :].unsqueeze(2).to_broadcast([nt, 4, 128]), op=ALU.mult), reads=[AB, constB], writes=[kdecB[t]])
                            pending.append((stepc["i"], lambda T_=T_, TB_=TB_, t=t: transposes(
                                [(T_[:, 0, :], 4, TB_, krT[:, :, t * 128:t * 128 + nt], [krTB[h] for h in range(4)])], nt, "act")))
                    elif b == 2:
                        S.act(lambda bk=bk, t=t: nc.scalar.copy(out=vr[:nt, t, :], in_=banks[bk][:nt, :]), reads=[bankB[bk]], writes=[vrB[t]])
                    elif b == 3:
                        S.act(lambda bk=bk, t=t: nc.scalar.activation(out=sgr[:nt, t, :], in_=banks[bk][:nt, :], func=AF.Silu), reads=[bankB[bk]], writes=[sgrB[t]])
                    elif b in (4, 5):
                        if b == 4:
                            ai = rr("f32a", 2); A, AB = f32a[ai], f32aB[ai]
                        else:
                            ai = rr("kdf", 2); A, AB = kdf[ai], kdfB[ai]
                        S.act(lambda bk=bk, A=A: nc.scalar.copy(out=A[:nt, :], in_=banks[bk][:nt, :]), reads=[bankB[bk]], writes=[AB])
                        rope_partial(A, AB, nt, t)
                        ti = rr("tb", 3); T_, TB_ = tb[ti], tbB[ti]
                        S.act(lambda A=A, T_=T_: nc.scalar.copy(out=T_[:nt, 0, :], in_=A[:nt, :]), reads=[AB], writes=[TB_])
                        if b == 4:
                            pending.append((stepc["i"], lambda T_=T_, TB_=TB_, t=t: q_transposes(T_, TB_, t)))
                        else:
                            S.dma(S.qsync, kout[t * 128:t * 128 + nt, :], A[:nt, :], reads=[AB], writes=[])
                            pending.append((stepc["i"], lambda T_=T_, TB_=TB_, kt=kt: transposes(
                                [(T_[:, 0, :], 4, TB_, KtH[:, :, kt * 128:kt * 128 + nt], [KtB[kt]])], nt, "act")))
                    else:
                        ai = 0; A, AB = vdf[ai], vdfB[ai]
                        S.act(lambda bk=bk, A=A: nc.scalar.copy(out=A[:nt, :], in_=banks[bk][:nt, :]), reads=[bankB[bk]], writes=[AB])
                        S.dma(S.qsync, vout[t * 128:t * 128 + nt, :], A[:nt, :], reads=[AB], writes=[])
                        S.dve(lambda A=A, kt=kt: nc.vector.tensor_copy(out=Vaug[:nt, kt, :, 0:128], in_=A[:nt, :].rearrange("p (h d) -> p h d", h=4)),
                              reads=[AB], writes=[VB[kt]])
            flush(True)

            while ret_queue:
                ret_queue.pop(0)()
            if retout is not None:
                S.dma(S.qsync, retout.rearrange("h d e -> d h e"), Sst[:, :, :], reads=[SstB], writes=[])

            nkt = kbase + ntiles
            W = ntok
            OTb = (4, 5)
            SMb = (6, 7)
            for h in range(4):
                if is_sample:
                    kts_list = [list(range(k0, min(k0 + 8, kbase))) for k0 in range(0, kbase, 8)] + [[kbase]]
                else:
                    kts_list = [[kt] for kt in range(nkt)]
                steps = [(kts, c) for kts in kts_list for c in range(2)]
                info = {}
                nmm = {0: 0, 1: 0}
                for kts, c in steps:
                    nmm[c] += len(kts)
                cntc = {0: 0, 1: 0}

                def emit_st(kts, c):
                    kt0 = kts[0]
                    lt = kt0 - kbase
                    nk = nt if (lt == ntiles - 1) else 128
                    qlo = 0 if (is_sample or lt < 0) else lt
                    ncol = ntok - qlo * 128
                    diag = lt >= 0 and not is_sample
                    bk = rr("sbank", 4)
                    qB = [QdTB[h]] if c == 0 else xnB[qlo:ntiles]
                    for j, kt in enumerate(kts):
                        S.pe(lambda j=j, kt=kt: nc.tensor.matmul(banks[bk][:nk, j * ncol:(j + 1) * ncol], KtH[:, h, kt * 128:kt * 128 + nk],
                                                                 Qp[c][:, h, qlo * 128:ntok], start=True, stop=True),
                             reads=[KtB[kt]] + qB, writes=[bankB[bk]])
                    wtot = ncol * len(kts)
                    if diag:
                        pi = rr("pTd", 3)
                        P_, PB_ = pTd[pi], pTdB[pi]
                        S.act(lambda: nc.scalar.activation(out=P_[:, 64:ncol], in_=banks[bk][:, 64:ncol], func=AF.Exp, scale=0.125),
                              reads=[bankB[bk]], writes=[PB_])
                        S.act(lambda: nc.scalar.activation(out=P_[0:64, 0:64], in_=banks[bk][0:64, 0:64], func=AF.Exp, scale=0.125),
                              reads=[bankB[bk]], writes=[PB_])
                    else:
                        pi = rr("pT", 3)
                        P_, PB_ = pT[pi], pTB[pi]
                        S.act(lambda: nc.scalar.activation(out=P_[:nk, 0:wtot], in_=banks[bk][:nk, 0:wtot], func=AF.Exp, scale=0.125),
                              reads=[bankB[bk]], writes=[PB_])
                    info[(kts[0], c)] = (nk, qlo, ncol, P_, PB_)

                def emit_pv(kts, c):
                    nk, qlo, ncol, P_, PB_ = info.pop((kts[0], c))
                    for j, kt in enumerate(kts):
                        first = cntc[c] == 0
                        lastf = cntc[c] == nmm[c] - 1
                        cntc[c] += 1
                        S.pe(lambda j=j, kt=kt, first=first, lastf=lastf: nc.tensor.matmul(
                            banks[OTb[c]][:, qlo * 128:ntok], Vaug[:nk, kt, h, 0:128], P_[:nk, j * ncol:(j + 1) * ncol], start=first, stop=lastf),
                            reads=[PB_, VB[kt]], writes=[bankB[OTb[c]]])
                        if is_sample:
                            S.pe(lambda j=j, first=first, lastf=lastf: nc.tensor.matmul(
                                banks[SMb[c]][:, qlo * 128:ntok], ones_b[:nk, :], P_[:nk, j * ncol:(j + 1) * ncol], start=first, stop=lastf),
                                reads=[PB_, constB], writes=[bankB[SMb[c]]])
                        else:
                            PA, PAB = f32a[c], f32aB[c]
                            if first:
                                S.dve(lambda PA=PA: nc.vector.tensor_copy(out=PA[:, qlo * 128:ntok], in_=P_[:, 0:ncol]), reads=[PB_], writes=[PAB])
                            else:
                                S.dve(lambda PA=PA: nc.vector.tensor_tensor(out=PA[:, qlo * 128:ntok], in0=PA[:, qlo * 128:ntok], in1=P_[:, 0:ncol], op=ALU.add),
                                      reads=[PB_, PAB], writes=[PAB])

                DEPTH = 2
                for i in range(len(steps) + DEPTH):
                    if i < len(steps):
                        emit_st(*steps[i])
                    if i >= DEPTH:
                        emit_pv(*steps[i - DEPTH])
                A0, A1, A0B, A1B = f32a[0], f32a[1], f32aB[0], f32aB[1]
                if not is_sample:
                    for c_ in range(2):
                        ti_ = rr("tb", 3)
                        PBF, PBFB = tb[ti_][:, 0, :], tbB[ti_]
                        S.dve(lambda c_=c_, PBF=PBF: nc.vector.tensor_copy(out=PBF[:, 0:W], in_=f32a[c_][:, 0:W]), reads=[f32aB[c_]], writes=[PBFB])
                        S.pe(lambda c_=c_, PBF=PBF: nc.tensor.matmul(banks[SMb[c_]][:, 0:W], ones_b[:, :], PBF[:, 0:W], start=True, stop=True),
                             reads=[PBFB, constB], writes=[bankB[SMb[c_]]])
                S.dve(lambda: nc.vector.tensor_copy(out=A0[:, 0:W], in_=banks[SMb[0]][:, 0:W]), reads=[bankB[SMb[0]]], writes=[A0B])
                S.dve(lambda: nc.vector.tensor_copy(out=A1[:, 0:W], in_=banks[SMb[1]][:, 0:W]), reads=[bankB[SMb[1]]], writes=[A1B])
                S.dve(lambda: nc.vector.tensor_tensor(out=f32b[:, 0:W], in0=banks[OTb[0]][:, 0:W], in1=A1[:, 0:W], op=ALU.mult), reads=[bankB[OTb[0]], A1B], writes=[f32bB])
                S.dve(lambda: nc.vector.tensor_tensor(out=f32c[:, 0:W], in0=banks[OTb[1]][:, 0:W], in1=A0[:, 0:W], op=ALU.mult), reads=[bankB[OTb[1]], A0B], writes=[f32cB])
                S.dve(lambda: nc.vector.scalar_tensor_tensor(out=f32b[:, 0:W], in0=f32c[:, 0:W], scalar=nlam[:, 0:1], in1=f32b[:, 0:W], op0=ALU.mult, op1=ALU.add),
                      reads=[f32bB, f32cB, constB], writes=[f32bB])
                S.pool(lambda: nc.gpsimd.tensor_tensor(out=A0[:, 0:W], in0=A0[:, 0:W], in1=A1[:, 0:W], op=ALU.mult), reads=[A0B, A1B], writes=[A0B])
                S.pool(lambda: nc.gpsimd.tensor_tensor(out=A0[:, 0:W], in0=A0[:, 0:W], in1=A0[:, 0:W], op=ALU.mult), reads=[A0B], writes=[A0B])
                ti = rr("tb", 3)
                SQ, SQB = tb[ti][:, 0, :], tbB[ti]
                S.act(lambda: nc.scalar.activation(out=SQ[:, 0:W], in_=f32b[:, 0:W], func=AF.Square), reads=[f32bB], writes=[SQB])
                bk = rr("sbank", 4)
                S.pe(lambda: nc.tensor.matmul(banks[bk][:, 0:W], ones_b[:, :], SQ[:, 0:W], start=True, stop=True), reads=[SQB, constB], writes=[bankB[bk]])
                S.dve(lambda: nc.vector.scalar_tensor_tensor(out=f32c[:, 0:W], in0=A0[:, 0:W], scalar=EPS * 128.0, in1=banks[bk][:, 0:W], op0=ALU.mult, op1=ALU.add),
                      reads=[A0B, bankB[bk]], writes=[f32cB])
                S.act(lambda: nc.scalar.activation(out=f32c[:, 0:W], in_=f32c[:, 0:W], func=AF.Ln, scale=1.0 / 128), reads=[f32cB], writes=[f32cB])
                S.act(lambda: nc.scalar.activation(out=f32c[:, 0:W], in_=f32c[:, 0:W], func=AF.Exp, scale=-0.5), reads=[f32cB], writes=[f32cB])
                S.dve(lambda: nc.vector.scalar_tensor_tensor(out=mixT[:, 4 + h, 0:W], in0=f32b[:, 0:W], scalar=dg_sb[:, 0:1], in1=f32c[:, 0:W], op0=ALU.mult, op1=ALU.mult),
                      reads=[f32bB, f32cB, constB], writes=[mdB[h]])

            for t in range(ntiles):
                transposes([(mix[:, t, 0:512], 4, mixB[t], mixT[:, 0:4, t * 128:t * 128 + nt], [mixTB[t]])], nt)
            get_unit(ubase + 7)
            for t in range(ntiles):
                for o in range(2):
                    slot = (ubase + 7 + o) % NSLOT
                    wv = wsl[slot][:, :].rearrange("p (kk c) -> p kk c", kk=4)
                    for kk in range(4):
                        kc = 4 * o + kk
                        for hh in range(2):
                            bk = 2 * t + hh
                            S.pe(lambda kk=kk, kc=kc, t=t, hh=hh, bk=bk, wv=wv: nc.tensor.matmul(
                                banks[bk][:nt, :], mixT[:, kc, t * 128:t * 128 + nt], wv[:, kk, hh * 512:(hh + 1) * 512],
                                start=(kc == 0), stop=(kc == NKC - 1)),
                                reads=[wB[slot], mixTB[t] if kc < 4 else mdB[kc - 4]], writes=[bankB[bk]])
            postnorm_residual(ntiles, nt, 1, 1.0)

        def group(gi, xin, yout, kout, vout, tab_ap, ntiles, nt, kbase, is_sample, retout, mid_hook=None):
            ubase = gi * NU
            for t in range(ntiles):
                S.dma(S.qsync, x[:nt, t, :], xin[t * 128:t * 128 + nt, :], writes=[xB[t]])
            S.dma(S.qsync, tab[:nt, 0:ntiles, :], tab_ap.rearrange("(t p) c -> p t c", p=nt), writes=[tabB])
            ffn(ntiles, nt, ubase, 0)
            mixer(ntiles, nt, ubase + 22, kbase, is_sample, kout, vout, retout)
            inter = mid_hook() if mid_hook is not None else None
            ffn(ntiles, nt, ubase + 31, 2, inter)
            for t in range(ntiles):
                S.dma(S.qsync, yout[t * 128:t * 128 + nt, :], x[:nt, t, :], reads=[xB[t]], writes=[])

        stgB = [Buf("stg0"), Buf("stg1")]

        def sample_prefetch_early():
            S.dma(S.qsync, Sst[:, :, :], sret.rearrange("h d e -> d h e"), writes=[SstB])
            S.act(lambda: nc.scalar.copy(out=Sb[:, :, :], in_=Sst[:, :, :]), reads=[SstB], writes=[SbB])
            for kt_ in range(KT_P):
                S.dma(S.qpool, Vaug[:, kt_, :, 0:128],
                      cv[kt_ * 128:(kt_ + 1) * 128, :].rearrange("p (h d) -> p h d", h=4), writes=[VB[kt_]])
            for bi in range(min(2, knb)):
                kload(bi)
            return None

        kstg = (mix[:, 0:4, 0:512], mix[:, 0:4, 512:1024])
        knb = (KT_P + 3) // 4

        def kload(bi):
            k4 = bi * 4
            n4 = min(4, KT_P - k4)
            S.dma(S.qpool, kstg[bi % 2][:, 0:n4, :], ck[k4 * 128:(k4 + n4) * 128, :].rearrange("(k p) c -> p k c", p=128),
                  writes=[stgB[bi % 2]] + (mixB if bi < 2 else []))

        def sample_prefetch_k():
            for bi in range(knb):
                k4 = bi * 4
                n4 = min(4, KT_P - k4)
                for i in range(n4):
                    bk = rr("bank", 8)
                    src = kstg[bi % 2][:, i, :]
                    for j in range(4):
                        S.pe(lambda j=j, bk=bk, src=src: nc.tensor.transpose(bank16[bk][:, j * 128:(j + 1) * 128], src[:, j * 128:(j + 1) * 128], ident[:, :]),
                             reads=[stgB[bi % 2], constB] + mixB, writes=[bankB[bk]])
                    kk = k4 + i
                    S.dve(lambda bk=bk, kk=kk: nc.vector.tensor_copy(out=KtH[:, :, kk * 128:(kk + 1) * 128],
                                                                     in_=bank16[bk][:, 0:512].rearrange("p (k c) -> p k c", k=4)),
                          reads=[bankB[bk]], writes=[KtB[kk]])
                if bi + 2 < knb:
                    kload(bi + 2)

        S.dve(lambda: nc.vector.memset(st[:], 0.0), writes=[stB, stQ_g] + stT + stP)
        S.dve(lambda: nc.vector.memset(Sst[:], 0.0), writes=[SstB])
        S.dve(lambda: nc.vector.memset(Sb[:], 0.0), writes=[SbB])
        for g in range(NG):
            r0 = g * 512
            group(g, xp[r0:r0 + 512, :], yp[r0:r0 + 512, :], kp[r0:r0 + 512, :], vp[r0:r0 + 512, :], tabp[r0:r0 + 512, :],
                  4, 128, 4 * g, False, retp if g == NG - 1 else None, sample_prefetch_early if g == NG - 1 else None)
        sample_prefetch_k()
        group(NG, xs_in, ys, ks, vs, tabs, 1, 64, KT_P, True, rets)
        S.finish()
    return nc


def _tables(NP, PAST):
    def rope_tab(pos, theta, rot_dim):
        half = rot_dim // 2
        inv = np.power(np.float64(theta), -np.arange(half, dtype=np.float64) * (2.0 / rot_dim))
        ang = pos.astype(np.float64)[:, None] * inv[None, :]
        return np.cos(ang), np.sin(ang)

    def tab(pos):
        cr, sr = rope_tab(pos, 10000.0, 128)
        cd, sd = rope_tab(pos, 500000.0, 16)
        s = 128.0 ** -0.5
        return np.ascontiguousarray(np.concatenate([cr, sr, cr * s, sr * s, cd, sd], axis=1).astype(np.float32))

    tabp = tab(np.arange(NP))
    tabs = tab(PAST + np.arange(64))
    log_g = np.log(1.0 - np.power(2.0, -5.0 - np.arange(4, dtype=np.float64)))
    idx = np.arange(128, dtype=np.float64)
    rel = idx[None, :] - idx[:, None]
    dm = np.where(rel >= 0, np.exp(log_g[:, None, None] * np.maximum(rel, 0.0)), 0.0)
    dmaskT = np.ascontiguousarray(dm.transpose(1, 0, 2).reshape(128, 512).astype(np.float32))
    cst = np.zeros((128, 20), np.float32)
    cst[:, 0:4] = np.exp(log_g[None, :] * (idx + 1.0)[:, None])
    cst[:, 4:8] = np.exp(log_g[None, :] * (127.0 - idx)[:, None])
    cst[:, 8:12] = np.exp(log_g[None, :] * np.maximum(63.0 - idx, 0.0)[:, None])
    cst[:, 12:16] = np.exp(log_g * 128.0)[None, :]
    cst[:, 16:20] = np.exp(log_g * 64.0)[None, :]
    ident = np.eye(128, dtype=np.float32)
    return tabp, tabs, dmaskT, cst, ident


_CACHE = {}


def run(NP, PAST, inputs, n_cores):
    f = lambda a: np.ascontiguousarray(np.asarray(a, dtype=np.float32))
    tabp, tabs, dmaskT, cst, ident = _tables(NP, PAST)
    shared = {
        "wg1": f(inputs["ffn1_w_gate"][0]), "wu1": f(inputs["ffn1_w_up"][0]), "wd1": f(inputs["ffn1_w_down"][0]),
        "wg2": f(inputs["ffn2_w_gate"][0]), "wu2": f(inputs["ffn2_w_up"][0]), "wd2": f(inputs["ffn2_w_down"][0]),
        "win": f(inputs["w_in"][0]), "wout": f(inputs["w_out"][0]),
        "gpre": f(np.stack([inputs["ffn1_pre_g"][0], inputs["mix_pre_g"][0], inputs["ffn2_pre_g"][0]]).reshape(3, NKC, 128).transpose(2, 0, 1)),
        "gpost": f(np.broadcast_to(np.stack([inputs["ffn1_post_g"][0], inputs["mix_post_g"][0], inputs["ffn2_post_g"][0]])[:, None, :], (3, 128, D))),
        "retg": f(np.broadcast_to(np.asarray(inputs["ret_norm_g"][0]).reshape(1, 512), (128, 512))),
        "dg": f(np.asarray(inputs["diff_norm_g"][0]).reshape(128, 1)),
        "lvec": f(np.broadcast_to(np.concatenate([np.asarray(inputs[k][0]) for k in ("diff_lq1", "diff_lk1", "diff_lq2", "diff_lk2")]).reshape(1, 256), (128, 256))),
        "tabp": tabp, "tabs": tabs, "dmask": dmaskT, "cst": cst, "ident": ident,
    }
    in_maps = []
    for c in range(n_cores):
        m = dict(shared)
        m["xp"] = f(inputs["x_prompt"][c]); m["xs"] = f(inputs["x_sample"][c])
        m["sret"] = f(inputs["state_ret"][0, c])
        m["ck"] = f(np.asarray(inputs["cache_diff_k"][0, c]).reshape(PAST, 512))
        m["cv"] = f(np.asarray(inputs["cache_diff_v"][0, c]).reshape(PAST, 512))
        in_maps.append(m)
    key = (NP, PAST)
    nc = build(NP, PAST)
    res = run_bass_kernel_spmd(nc, in_maps, core_ids=list(range(n_cores)))
    R = res.results
    st = lambda k: np.stack([np.asarray(r[k], dtype=np.float32) for r in R])
    y_p = st("yp"); y_s = st("ys")
    ret_p = st("retp")[None]; k_p = st("kp").reshape(1, n_cores, NP, 4, 2, 64); v_p = st("vp").reshape(1, n_cores, NP, 4, 128)
    ret_s = st("rets")[None]; k_s = st("ks").reshape(1, n_cores, 64, 4, 2, 64); v_s = st("vs").reshape(1, n_cores, 64, 4, 128)
    return (y_p, y_s, ret_p, k_p, v_p, ret_s, k_s, v_s)


def kernel(**inputs):
    return run(4096, 4096, inputs, 8)
```
